# Optimizing a Trainium2 kernel written in Bass

```python
import jax, jax.numpy as jnp
from jax import lax
import numpy as np

D_MODEL = 2048
BATCH = 16
SEQ = 256
DEPTH = 4
DEC_BATCH = 4
DEC_SEQ = 1024
PAST_LEN = 256

GRID_W = 64
HEAD_DIM = 128
N_Q_HEADS = 8
N_KV_HEADS = 2
N_GROUP = N_Q_HEADS // N_KV_HEADS
ATTN_DIM = N_Q_HEADS * HEAD_DIM
KV_DIM = N_KV_HEADS * HEAD_DIM
Q_BLOCK = 128
ROPE_THETA = 10000.0
ROPE_AXIS_DIM = HEAD_DIM // 2
F_GROUPS = 8
F_GROUP_DIM = 128
F_DIM = F_GROUPS * F_GROUP_DIM
GLA_HEADS = 4
GLA_DK = 128
GLA_DV = 256
GLA_K_DIM = GLA_HEADS * GLA_DK
GLA_V_DIM = GLA_HEADS * GLA_DV
GLA_GATE_RANK = 16
GLA_GATE_TAU = 16.0
GLA_CHUNK = 64
D_FF = 5632
CONV_W = 3
N_MOD = 6
N_BRANCH = 3
EPS = 1e-6

IN_SPLITS = (F_DIM, ATTN_DIM, KV_DIM, KV_DIM, GLA_K_DIM, GLA_K_DIM, GLA_V_DIM, GLA_V_DIM,
             GLA_GATE_RANK, GLA_GATE_RANK, N_BRANCH * D_MODEL)
N_IN = F_DIM + ATTN_DIM + 2 * KV_DIM + 2 * GLA_K_DIM + 2 * GLA_V_DIM + 2 * GLA_GATE_RANK + N_BRANCH * D_MODEL

kernel_name = 'hybrid_fnet_gqa_gla_convffn_diffusion_step'

F32 = jnp.float32


def rms_norm(x, g):
    xf = x.astype(F32)
    y = xf * lax.rsqrt(jnp.mean(xf * xf, axis=-1, keepdims=True) + EPS)
    return (y * g.astype(F32)).astype(x.dtype)


def split_cols(z):
    idx = np.cumsum(np.array(IN_SPLITS))[:-1].tolist()
    return jnp.split(z, idx, axis=-1)


def grid_angles(T):
    rows = T // GRID_W
    row = jnp.repeat(jnp.arange(rows, dtype=F32), GRID_W)
    col = jnp.tile(jnp.arange(GRID_W, dtype=F32), rows)
    inv = ROPE_THETA ** (-jnp.arange(0, ROPE_AXIS_DIM, 2, dtype=F32) / ROPE_AXIS_DIM)
    return row[:, None] * inv, col[:, None] * inv


def _rotate(x, ang):
    cos = jnp.cos(ang)[None, :, None, :]
    sin = jnp.sin(ang)[None, :, None, :]
    x1, x2 = jnp.split(x, 2, axis=-1)
    return jnp.concatenate([x1 * cos - x2 * sin, x2 * cos + x1 * sin], axis=-1)


def axial_rope(x, ang_row, ang_col):
    xf = x.astype(F32)
    out = jnp.concatenate([_rotate(xf[..., :ROPE_AXIS_DIM], ang_row),
                           _rotate(xf[..., ROPE_AXIS_DIM:], ang_col)], axis=-1)
    return out.astype(x.dtype)


def block_attention(q, k, v):
    B, T = q.shape[0], q.shape[1]
    nb = T // Q_BLOCK
    qb = q.reshape(B, nb, Q_BLOCK, N_KV_HEADS, N_GROUP, HEAD_DIM).transpose(1, 0, 2, 3, 4, 5)
    kf = k.astype(F32)
    vf = v.astype(F32)
    scale = HEAD_DIM ** -0.5

    def one_block(qblk):
        s = jnp.einsum('bqkgd,bskd->bkgqs', qblk.astype(F32), kf) * scale
        p = jax.nn.softmax(s, axis=-1)
        return jnp.einsum('bkgqs,bskd->bqkgd', p, vf).astype(q.dtype)

    o = lax.map(one_block, qb)
    return o.transpose(1, 0, 2, 3, 4, 5).reshape(B, T, ATTN_DIM)


def gla_scan(q, k, v, log_a, s0):
    B, T = q.shape[0], q.shape[1]
    nc = T // GLA_CHUNK

    def chunks(a):
        return a.astype(F32).reshape(B, nc, GLA_CHUNK, GLA_HEADS, a.shape[-1]).transpose(1, 0, 3, 2, 4)

    mask = jnp.tril(jnp.ones((GLA_CHUNK, GLA_CHUNK), dtype=bool))

    def step(S, inp):
        qc, kc, vc, ac = inp
        b = jnp.cumsum(ac, axis=2)
        o_inter = jnp.einsum('bhtd,bhdv->bhtv', qc * jnp.exp(b), S)
        diff = b[:, :, :, None, :] - b[:, :, None, :, :]
        dec = jnp.exp(jnp.where(mask[:, :, None], diff, -jnp.inf))
        att = jnp.einsum('bhtd,bhsd,bhtsd->bhts', qc, kc, dec)
        o = o_inter + jnp.einsum('bhts,bhsv->bhtv', att, vc)
        b_last = b[:, :, -1:, :]
        S_new = jnp.exp(b_last[:, :, 0, :])[..., None] * S + jnp.einsum(
            'bhsd,bhsv->bhdv', kc * jnp.exp(b_last - b), vc)
        return S_new, o

    s_fin, o = lax.scan(step, s0.astype(F32), (chunks(q), chunks(k), chunks(v), chunks(log_a)))
    o = o.transpose(1, 0, 3, 2, 4).reshape(B, T, GLA_HEADS, GLA_DV)
    return o, s_fin


def gla_bidirectional(q, k, v, la_f, la_b, sf0, sb0):
    o_f, sf = gla_scan(q, k, v, la_f, sf0)
    o_b, sb = gla_scan(q[:, ::-1], k[:, ::-1], v[:, ::-1], la_b[:, ::-1], sb0)
    return o_f + o_b[:, ::-1], sf, sb


def token_mixers(h, lw, is_latent, ctx_k, ctx_v, sf0, sb0):
    B, T, _ = h.shape
    z = h @ lw['w_in']
    f_in, q, k, v, gq, gk, gv, gr, gaf, gab, gates = split_cols(z)

    fa = f_in.reshape(B, T, F_GROUPS, F_GROUP_DIM).astype(F32)
    fa = jnp.real(jnp.fft.fft2(fa, axes=(1, 3), norm='ortho')).reshape(B, T, F_DIM).astype(h.dtype)
    br_a = fa @ lw['w_fourier']

    q = rms_norm(q.reshape(B, T, N_Q_HEADS, HEAD_DIM), lw['q_norm'])
    k = rms_norm(k.reshape(B, T, N_KV_HEADS, HEAD_DIM), lw['k_norm'])
    v = v.reshape(B, T, N_KV_HEADS, HEAD_DIM)
    if is_latent:
        ang_r, ang_c = grid_angles(T)
        q = axial_rope(q, ang_r, ang_c)
        keys = jnp.concatenate([ctx_k.astype(k.dtype), axial_rope(k, ang_r, ang_c)], axis=1)
        vals = jnp.concatenate([ctx_v.astype(v.dtype), v], axis=1)
    else:
        keys, vals = k, v
    br_b = block_attention(q, keys, vals) @ lw['w_attn']

    gq = gq.reshape(B, T, GLA_HEADS, GLA_DK) * (GLA_DK ** -0.5)
    gk = gk.reshape(B, T, GLA_HEADS, GLA_DK)
    gv = gv.reshape(B, T, GLA_HEADS, GLA_DV)
    la_f = jax.nn.log_sigmoid((gaf @ lw['w_gate_f'] + lw['b_gate_f']).astype(F32)) / GLA_GATE_TAU
    la_b = jax.nn.log_sigmoid((gab @ lw['w_gate_b'] + lw['b_gate_b']).astype(F32)) / GLA_GATE_TAU
    la_f = la_f.reshape(B, T, GLA_HEADS, GLA_DK)
    la_b = la_b.reshape(B, T, GLA_HEADS, GLA_DK)
    o, sf, sb = gla_bidirectional(gq, gk, gv, la_f, la_b, sf0, sb0)
    o = rms_norm(o, lw['gla_norm']).reshape(B, T, GLA_V_DIM).astype(h.dtype) * jax.nn.silu(gr)
    br_c = o @ lw['w_gla']

    g_a, g_b, g_c = jnp.split(gates, N_BRANCH, axis=-1)
    merged = jax.nn.sigmoid(g_a) * br_a + jax.nn.sigmoid(g_b) * br_b + jax.nn.sigmoid(g_c) * br_c
    return merged @ lw['w_out'], k, v, sf, sb


def conv_ffn(h, lw):
    u = h @ lw['w_up']
    up = jnp.pad(u, ((0, 0), (1, 1), (0, 0)))
    cw = lw['conv_w']
    u = up[:, :-2] * cw[0] + up[:, 1:-1] * cw[1] + up[:, 2:] * cw[2] + lw['conv_b']
    g, val = jnp.split(u, 2, axis=-1)
    return (jax.nn.silu(g) * val) @ lw['w_down']


def trunk_layer(x, mod, lw, is_latent, ctx_k, ctx_v, sf0, sb0):
    shift1, scale1, gate1, shift2, scale2, gate2 = jnp.split(mod, N_MOD, axis=-1)
    h = rms_norm(x, lw['norm1']) * (1 + scale1) + shift1
    mix, k, v, sf, sb = token_mixers(h, lw, is_latent, ctx_k, ctx_v, sf0, sb0)
    x = x + gate1 * mix
    h = rms_norm(x, lw['norm2']) * (1 + scale2) + shift2
    x = x + gate2 * conv_ffn(h, lw)
    return x, k, v, sf, sb


def setup_inputs(seed: int = 0) -> dict:
    key = jax.random.key(seed)
    ks = iter(jax.random.split(key, 40))
    nrm = lambda shape, s: jax.random.normal(next(ks), shape, F32) * s
    gain = lambda shape: 1.0 + 0.01 * jax.random.normal(next(ks), shape, F32)
    return {
        'x_prompt': nrm((BATCH, SEQ, D_MODEL), 1.0),
        'x_sample': nrm((DEC_BATCH, DEC_SEQ, D_MODEL), 1.0),
        'cache_k': nrm((DEC_BATCH, DEPTH, PAST_LEN, N_KV_HEADS, HEAD_DIM), 1.0),
        'cache_v': nrm((DEC_BATCH, DEPTH, PAST_LEN, N_KV_HEADS, HEAD_DIM), 1.0),
        'state_gla_fwd': nrm((DEC_BATCH, DEPTH, GLA_HEADS, GLA_DK, GLA_DV), 2.0),
        'state_gla_bwd': nrm((DEC_BATCH, DEPTH, GLA_HEADS, GLA_DK, GLA_DV), 2.0),
        'c': nrm((DEC_BATCH, D_MODEL), 1.0),
        'c_ctx': nrm((D_MODEL,), 1.0),
        'w_ada': nrm((DEPTH, D_MODEL, N_MOD * D_MODEL), 0.5 * D_MODEL ** -0.5),
        'b_ada': nrm((DEPTH, N_MOD * D_MODEL), 0.01),
        'norm1': gain((DEPTH, D_MODEL)),
        'w_in': nrm((DEPTH, D_MODEL, N_IN), D_MODEL ** -0.5),
        'q_norm': gain((DEPTH, HEAD_DIM)),
        'k_norm': gain((DEPTH, HEAD_DIM)),
        'w_fourier': nrm((DEPTH, F_DIM, D_MODEL), F_DIM ** -0.5),
        'w_attn': nrm((DEPTH, ATTN_DIM, D_MODEL), ATTN_DIM ** -0.5),
        'w_gate_f': nrm((DEPTH, GLA_GATE_RANK, GLA_K_DIM), GLA_GATE_RANK ** -0.5),
        'b_gate_f': nrm((DEPTH, GLA_K_DIM), 0.01),
        'w_gate_b': nrm((DEPTH, GLA_GATE_RANK, GLA_K_DIM), GLA_GATE_RANK ** -0.5),
        'b_gate_b': nrm((DEPTH, GLA_K_DIM), 0.01),
        'gla_norm': gain((DEPTH, GLA_DV)),
        'w_gla': nrm((DEPTH, GLA_V_DIM, D_MODEL), GLA_V_DIM ** -0.5),
        'w_out': nrm((DEPTH, D_MODEL, D_MODEL), D_MODEL ** -0.5),
        'norm2': gain((DEPTH, D_MODEL)),
        'w_up': nrm((DEPTH, D_MODEL, 2 * D_FF), D_MODEL ** -0.5),
        'conv_w': nrm((DEPTH, CONV_W, 2 * D_FF), CONV_W ** -0.5),
        'conv_b': nrm((DEPTH, 2 * D_FF), 0.01),
        'w_down': nrm((DEPTH, D_FF, D_MODEL), D_FF ** -0.5),
        'final_norm': gain((D_MODEL,)),
    }


def reference(x_prompt, x_sample, cache_k, cache_v, state_gla_fwd, state_gla_bwd, c, c_ctx,
              w_ada, b_ada, norm1, w_in, q_norm, k_norm, w_fourier, w_attn, w_gate_f, b_gate_f,
              w_gate_b, b_gate_b, gla_norm, w_gla, w_out, norm2, w_up, conv_w, conv_b, w_down,
              final_norm):
    xp = x_prompt
    xs = x_sample
    zero_state = jnp.zeros((xp.shape[0], GLA_HEADS, GLA_DK, GLA_DV), F32)
    new_k, new_v, new_sf, new_sb = [], [], [], []
    for l in range(DEPTH):
        lw = {
            'norm1': norm1[l], 'w_in': w_in[l], 'q_norm': q_norm[l], 'k_norm': k_norm[l],
            'w_fourier': w_fourier[l], 'w_attn': w_attn[l], 'w_gate_f': w_gate_f[l],
            'b_gate_f': b_gate_f[l], 'w_gate_b': w_gate_b[l], 'b_gate_b': b_gate_b[l],
            'gla_norm': gla_norm[l], 'w_gla': w_gla[l], 'w_out': w_out[l], 'norm2': norm2[l],
            'w_up': w_up[l], 'conv_w': conv_w[l], 'conv_b': conv_b[l], 'w_down': w_down[l],
        }
        mod_ctx = (jax.nn.silu(c_ctx) @ w_ada[l] + b_ada[l])[None, None, :]
        xp, k_ctx, v_ctx, sf, sb = trunk_layer(xp, mod_ctx, lw, False, None, None, zero_state, zero_state)
        new_k.append(k_ctx)
        new_v.append(v_ctx)
        new_sf.append(sf.astype(xp.dtype))
        new_sb.append(sb.astype(xp.dtype))
        mod_lat = (jax.nn.silu(c) @ w_ada[l] + b_ada[l])[:, None, :]
        xs, _, _, _, _ = trunk_layer(xs, mod_lat, lw, True, cache_k[:, l], cache_v[:, l],
                                     state_gla_fwd[:, l], state_gla_bwd[:, l])
    y_prompt = rms_norm(xp, final_norm)
    y_sample = rms_norm(xs, final_norm)
    new_cache_k = jnp.stack(new_k, axis=1)
    new_cache_v = jnp.stack(new_v, axis=1)
    new_state_gla_fwd = jnp.stack(new_sf, axis=1)
    new_state_gla_bwd = jnp.stack(new_sb, axis=1)
    return (y_prompt, y_sample, new_cache_k, new_cache_v, new_state_gla_fwd, new_state_gla_bwd)
```

```python
import os
import numpy as np
import concourse.bass as bass
import concourse.mybir as mybir
from concourse.bass_utils import run_bass_kernel_spmd
from contextlib import ExitStack

F32 = mybir.dt.float32
BF16 = mybir.dt.bfloat16
AF = mybir.ActivationFunctionType
ALU = mybir.AluOpType

D = 2048
KD = 16
NT = 1024
NSEG = 4
SEG = 256
DEPTH = 4
D_FF = 5632
NJ = 44
N_IN = 11808
C_F, C_Q, C_K, C_V, C_GQ, C_GK, C_GV, C_GR, C_GAF, C_GAB, C_GT = (
    0, 1024, 2048, 2304, 2560, 3072, 3584, 4608, 5632, 5648, 5664)
EPS = 1e-6
WSLOT = 4096
NWS = 3
ND = 24

V_N1, V_N2, V_BADA, V_CW, V_CB, V_QN, V_KN, V_GN, V_FN = 0, 16, 32, 128, 392, 480, 481, 482, 484
NV = 500
B_ID, B_ONE, B_MF, B_MB, B_TF, B_TB, B_T2F, B_T2B, B_RM, B_CS, B_COS, B_SIN = (
    0, 128, 256, 768, 1280, 1408, 1536, 1664, 1792, 1920, 2176, 3200)
NB = 4224
S_CARRY, S_AB, S_C, S_ID = 0, 1, 41, 57
NS = 185


class T:
    __slots__ = ("name", "w", "r", "ps")

    def __init__(self, name, after=(), ps=False):
        self.name = name
        self.w = None
        self.r = []
        self.ps = ps
        for t in after:
            if t.w is not None:
                self.r.append(t.w)
            self.r.extend(t.r)


class Op:
    __slots__ = ("eng", "fn", "deps", "sig", "val", "dma", "dsem", "dval")


class K:
    def __init__(self, nc, depth):
        self.nc = nc
        self.L = depth
        self.ops = {e: [] for e in ("pe", "act", "dve", "pool", "sp")}
        self.dma_last = [None] * ND
        self.dma_cnt = [0] * ND
        self.dma_rr = {"sp": 0, "pool": ND // 2, "act": 0}
        self.ps_free = list(range(8))
        self.ws_rr = 0
        self.mod_state = {}

    def op(self, eng, fn, reads=(), writes=(), dma=False):
        o = Op()
        o.eng, o.fn, o.dma, o.sig, o.val = eng, fn, dma, False, 0
        deps = {}
        for t in reads:
            if t.w is not None:
                deps[id(t.w)] = t.w
            if t.ps:
                for r in t.r:
                    if r.eng != eng:
                        deps[id(r)] = r
        for t in writes:
            if t.w is not None:
                deps[id(t.w)] = t.w
            for r in t.r:
                deps[id(r)] = r
        if dma:
            h = ND // 2
            s = self.dma_rr[eng]
            lo = h if eng == "pool" else 0
            self.dma_rr[eng] = lo + (s - lo + 1) % h
            if self.dma_last[s] is not None:
                deps[id(self.dma_last[s])] = self.dma_last[s]
            self.dma_cnt[s] += 16
            o.dsem, o.dval = s, self.dma_cnt[s]
            self.dma_last[s] = o
        dl = []
        for d in deps.values():
            if (not d.dma) and d.eng == "pe" and eng == "pe" and not dma:
                continue
            dl.append(d)
            d.sig = True
        o.deps = dl
        for t in reads:
            if dma:
                t.r.append(o)
            else:
                t.r = [x for x in t.r if x.dma or x.eng != eng] + [o]
        for t in writes:
            t.w = o
            t.r = []
        self.ops[eng].append(o)
        return o

    def mod_tiles(self, l, n, part):
        if l >= self.L or l < 0:
            return
        key = (l, part)
        st = self.mod_state.get(key)
        if st is None:
            st = self.mod_state[key] = {"bank": self.psum(), "next": 0 if part == 0 else 16}
        if st["bank"] is None:
            return
        bm = st["bank"]
        end = 16 if part == 0 else 48
        w_ada = self.io["w_ada"][l]
        SCB = self.A["SCB"]
        for _ in range(n):
            j2 = st["next"]
            if j2 >= end:
                return
            st["next"] += 1
            wv, wt = self.wload(self.wsrc(w_ada, j2 * 256, 256), KD, 256)
            for mi in range(2):
                j = j2 * 2 + mi
                pairs = [(wv[:, k, mi * 128:(mi + 1) * 128], SCB[:, k:k + 1]) for k in range(KD)]
                self.mm(self.PS[:, bm, j:j + 1], pairs, [wt, self.tSCB], self.PT[bm])

    def mod_finish(self, l, part):
        self.mod_tiles(l, 48, part)
        st = self.mod_state[(l, part)]
        bm = st["bank"]
        MOD, VEC = self.A["MOD"], self.A["VEC"]
        tMOD, tVEC = self.tMODx, self.tVECx
        c0, c1 = (0, 32) if part == 0 else (32, 96)
        self.tt(MOD[:, c0:c1], self.PS[:, bm, c0:c1], VEC[:, V_BADA + c0:V_BADA + c1], ALU.add, [self.PT[bm], tVEC], [tMOD])
        self.psfree(bm)
        st["bank"] = None
        if part == 0:
            self.stt(MOD[:, 96:112], MOD[:, 16:32], 1.0, VEC[:, V_N1:V_N1 + 16], ALU.add, ALU.mult, [tMOD, tVEC], [tMOD])
        else:
            self.stt(MOD[:, 112:128], MOD[:, 64:80], 1.0, VEC[:, V_N2:V_N2 + 16], ALU.add, ALU.mult, [tMOD, tVEC], [tMOD])

    def psum(self, n=1):
        if n == 1:
            b = self.ps_free.pop(0)
            return b
        for i, b in enumerate(self.ps_free):
            if b % 2 == 0 and (b + 1) in self.ps_free:
                self.ps_free.remove(b)
                self.ps_free.remove(b + 1)
                return b
        raise RuntimeError("no psum pair")

    def psfree(self, b, n=1):
        for i in range(n):
            self.ps_free.append(b + i)

    def emit(self, block, sems, dsems):
        for e in self.ops:
            c = 0
            for o in self.ops[e]:
                if o.sig and not o.dma:
                    c += 1
                    o.val = c
        engs = {"pe": "tensor", "act": "scalar", "dve": "vector", "pool": "gpsimd", "sp": "sync"}
        for e, bn in engs.items():
            ops = self.ops[e]
            final = (e == "sp")

            def body(eng, ops=ops, e=e, final=final):
                waited = {}
                for o in ops:
                    for d in o.deps:
                        if d.dma:
                            key, sem, val = ("d", d.dsem), dsems[d.dsem], d.dval
                        else:
                            key, sem, val = ("e", d.eng), sems[d.eng], d.val
                        if waited.get(key, 0) < val:
                            eng.wait_ge(sem, val)
                            waited[key] = val
                    ins = o.fn(eng)
                    if o.dma:
                        ins.then_inc(dsems[o.dsem], 16)
                    elif o.sig:
                        ins.then_inc(sems[e], 1)
                if final:
                    for s in range(ND):
                        if self.dma_cnt[s] > 0:
                            eng.wait_ge(dsems[s], self.dma_cnt[s])
                    for en in ("pe", "act", "dve"):
                        n = sum(1 for o in self.ops[en] if o.sig)
                        if n:
                            eng.wait_ge(sems[en], n)

            getattr(block, bn)(body)

    def mm(self, out, pairs, reads, pst, start=True, stop=True):
        n = len(pairs)

        def fn(t):
            ins = None
            for i, (l, r) in enumerate(pairs):
                ins = t.matmul(out, lhsT=l, rhs=r, start=(start and i == 0), stop=(stop and i == n - 1))
            return ins
        return self.op("pe", fn, reads, [pst])

    def act(self, out, in_, func, reads, writes, bias=None, scale=None):
        kw = {}
        if bias is not None:
            kw["bias"] = bias
        if scale is not None:
            kw["scale"] = scale
        return self.op("act", lambda a: a.activation(out=out, in_=in_, func=func, **kw), reads, writes)

    def tt(self, out, a, b, op, reads, writes, eng="dve"):
        return self.op(eng, lambda v: v.tensor_tensor(out=out, in0=a, in1=b, op=op), reads, writes)

    def ts(self, out, a, s1, s2, op0, op1, reads, writes, eng="dve"):
        if op1 is None:
            return self.op(eng, lambda v: v.tensor_scalar(out=out, in0=a, scalar1=s1, scalar2=0.0, op0=op0, op1=ALU.add), reads, writes)
        return self.op(eng, lambda v: v.tensor_scalar(out=out, in0=a, scalar1=s1, scalar2=s2, op0=op0, op1=op1), reads, writes)

    def stt(self, out, a, s, b, op0, op1, reads, writes, eng="dve"):
        return self.op(eng, lambda v: v.scalar_tensor_tensor(out=out, in0=a, scalar=s, in1=b, op0=op0, op1=op1), reads, writes)

    def cp(self, out, in_, reads, writes, eng="dve"):
        if eng == "act":
            return self.op("act", lambda a: a.activation(out=out, in_=in_, func=AF.Copy), reads, writes)
        return self.op(eng, lambda v: v.tensor_copy(out=out, in_=in_), reads, writes)

    def dma(self, q, out, in_, reads, writes):
        return self.op(q, lambda e: e.dma_start(out=out, in_=in_), reads, writes, dma=True)

    def rstd(self, out, ps_ap, n, reads, writes):
        self.ts(out, ps_ap, 1.0 / n, EPS, ALU.mult, ALU.add, reads, writes)
        self.act(out, out, AF.Sqrt, writes, writes)
        self.op("dve", lambda v: v.reciprocal(out=out, in_=out), writes, writes)

    def wload(self, src_ap, kc, cols):
        s = self.ws_rr
        self.ws_rr = (self.ws_rr + 1) % NWS
        view = self.WS[s][:, 0:kc * cols].rearrange("p (k c) -> p k c", k=kc)
        self.dma("pool", view, src_ap, [], [self.WT[s]])
        return view, self.WT[s]

    def wsrc(self, w_ap, c0, cols, k0=0, kc=None):
        v = w_ap.rearrange("(k p) c -> p k c", p=128)
        if kc is None:
            return v[:, :, c0:c0 + cols]
        return v[:, k0:k0 + kc, c0:c0 + cols]

    def proj_fm(self, w_ap, c0, nm, rhs, rhs_tiles, evac, kc=KD, mper=2, mrows=128):
        m = 0
        while m < nm:
            g = min(mper, nm - m)
            wv, wt = self.wload(self.wsrc(w_ap, c0 + m * mrows, g * mrows, 0, kc), kc, g * mrows)
            for mi in range(g):
                for half in range(2):
                    b = self.psum()
                    pap = self.PS[0:mrows, b, :]
                    pairs = [(wv[:, k, mi * mrows:(mi + 1) * mrows], rhs(k, half)) for k in range(kc)]
                    self.mm(pap, pairs, [wt] + rhs_tiles, self.PT[b])
                    evac(m + mi, half, pap, self.PT[b])
                    self.psfree(b)
            m += g

    def proj_tm(self, w_ap, c0, cols, lhs, lhs_tiles, evac, kc=KD):
        wv, wt = self.wload(self.wsrc(w_ap, c0, cols, 0, kc), kc, cols)
        for tc in range(8):
            b = self.psum()
            pap = self.PS[:, b, 0:cols]
            pairs = [(lhs(k, tc), wv[:, k, :]) for k in range(kc)]
            self.mm(pap, pairs, [wt] + lhs_tiles, self.PT[b])
            evac(tc, pap, self.PT[b])
            self.psfree(b)

    def build(self, io):
        nc = self.nc
        L = self.L
        self.io = io
        A = self.A
        X, H = A["X"], A["H"]
        self.WS = [A["W%d" % i] for i in range(NWS)]
        self.WT = [T("W%d" % i) for i in range(NWS)]
        self.PS = A["PS"]
        self.PT = [T("PS%d" % i, ps=True) for i in range(8)]
        XT = [T("X%d" % k) for k in range(KD)]
        HT = [T("H%d" % k) for k in range(KD)]
        CB, CS_, VEC, MOD = A["CB"], A["CST"], A["VEC"], A["MOD"]
        tCB, tCS, tVEC, tMOD = T("CB"), T("CST"), T("VEC"), T("MOD")
        self.tCBx, self.tCSx, self.tVECx = tCB, tCS, tVEC
        self.tMODx = tMOD
        self.tKST, self.tVST = [T("kst0"), T("kst1")], [T("vst0"), T("vst1")]
        AR = A["AR"]
        self.arena_tiles = []

        def bf(off, n):
            return AR[:, off:off + n]

        def f32v(off, n):
            return AR[:, off:off + n].bitcast(F32)

        ident = CB[:, B_ID:B_ID + 128]
        ones = CB[:, B_ONE:B_ONE + 128]
        identf = CS_[:, S_ID:S_ID + 128]
        carry = CS_[:, S_CARRY:S_CARRY + 1]

        self.dma("sp", X[:, :, :], io["x"][:, :, :], [], XT)
        self.dma("pool", CB[:, :], io["cb16"][:, :], [], [tCB])
        self.dma("sp", CS_[:, :], io["cst"][:, :], [], [tCS])
        SCB = A["SCB"]
        tSCB = T("SCB")
        self.tSCB = tSCB
        self.act(SCB[:, :], CS_[:, S_C:S_C + 16], AF.Silu, [tCS], [tSCB])

        def hr(k, half):
            return H[:, k, half * 512:(half + 1) * 512]

        def hl(k, tc):
            return H[:, k, tc * 128:(tc + 1) * 128]

        STOP = int(os.environ.get("KSTOP", "99"))
        self.tMG = T("mg_dummy")
        for l in range(L):
            w_ada, w_in = io["w_ada"][l], io["w_in"][l]
            if STOP < 1:
                break
            self.dma("sp", VEC[:, :], io["vecs"][l], [], [tVEC])
            self.mod_finish(l, 0)

            if STOP < 2:
                break
            self.norm_mod(X, XT, H, HT, MOD, tMOD, 96, 0, T("ar_n1", self.arena_tiles))
            self.arena_tiles = []
            if STOP < 3:
                break
            self.gla(l, hr, hl, HT)
            self.mod_finish(l, 1)
            if STOP < 4:
                break
            self.fnet(l, hr, hl, HT)
            if STOP < 5:
                break
            self.attn(l, hr, hl, HT)
            if STOP < 6:
                break
            MG = self.MG

            def ev3(m, half, pap, pt):
                xs = X[:, m, half * 512:(half + 1) * 512]
                self.stt(xs, pap, MOD[:, 32 + m:33 + m], xs, ALU.mult, ALU.add, [pt, tMOD, XT[m]], [XT[m]])
            self.proj_fm(io["w_out"][l], 0, 16, lambda k, half: MG[:, k, half * 512:(half + 1) * 512], [self.tMG], ev3)
            if STOP < 7:
                break
            self.norm_mod(X, XT, H, HT, MOD, tMOD, 112, 48, T("ar_n2", self.arena_tiles + [self.tMG]))
            self.arena_tiles = []
            if STOP < 8:
                break
            self.ffn(l, hr, HT, X, XT, MOD, tMOD)

        self.final(X, XT)

    def norm_mod(self, X, XT, H, HT, MOD, tMOD, gcol, shcol, tar):
        AR = self.A["AR"]
        SQ = [AR[:, i * 512:(i + 1) * 512] for i in range(4)]
        tSQ = [T("sq%d" % i, [tar]) for i in range(4)]
        RS = AR[:, 2048:4096].bitcast(F32)
        tRS = T("rs", [tar])
        T32 = [AR[:, 4096 + i * 2048:4096 + (i + 1) * 2048].bitcast(F32) for i in range(2)]
        tT32 = [T("t32%d" % i, [tar]) for i in range(2)]
        ones = self.A["CB"][:, B_ONE:B_ONE + 128]
        for half in range(2):
            b = self.psum()
            for k in range(KD):
                i = k % 4
                self.act(SQ[i], X[:, k, half * 512:(half + 1) * 512], AF.Square, [XT[k]], [tSQ[i]])
                self.mm(self.PS[:, b, :], [(ones, SQ[i])], [tSQ[i], self.tCBx], self.PT[b], start=(k == 0), stop=(k == KD - 1))
            self.rstd(RS[:, half * 512:(half + 1) * 512], self.PS[:, b, :], D, [self.PT[b]], [tRS])
            self.psfree(b)
        for k in range(KD):
            i = k % 2
            self.stt(T32[i], X[:, k, :], MOD[:, gcol + k:gcol + k + 1], RS, ALU.mult, ALU.mult, [XT[k], tMOD, tRS], [tT32[i]])
            self.act(H[:, k, :], T32[i], AF.Identity, [tT32[i], tMOD], [HT[k]], bias=MOD[:, shcol + k:shcol + k + 1])
        self.arena_tiles = tSQ + [tRS] + tT32

    def gla(self, l, hr, hl, HT):
        io, A = self.io, self.A
        AR, CB, CS_, VEC, MOD = A["AR"], A["CB"], A["CST"], A["VEC"], A["MOD"]
        tCB, tCS, tVEC = self.tCBx, self.tCSx, self.tVECx
        w_in = io["w_in"][l]
        prev = self.arena_tiles
        ones = CB[:, B_ONE:B_ONE + 128]
        carry = CS_[:, S_CARRY:S_CARRY + 1]
        O = AR[:, 16384:24576].rearrange("p (c t) -> p c t", c=8)
        tO = [T("O%d" % c, prev) for c in range(8)]
        base = 24576
        GAF = AR[0:17, 32768:34816].rearrange("p (a t) -> p a t", a=2)
        GW = AR[0:17, 34816:35840].rearrange("p (a t) -> p a t", a=2)
        tGAF = [T("gaf0", prev), T("gaf1", prev)]
        tGW = T("gw", prev)
        self.dma("pool", GW[:, :, :], io["gw"][l], [], [tGW])
        self.dma("pool", AR[16:17, 32768:34816].rearrange("p (a t) -> p a t", a=2), io["gones"][:, :, :], [], tGAF)
        for dr in range(2):
            def evg(m, half, pap, pt, dr=dr):
                self.cp(GAF[0:16, dr, half * 512:(half + 1) * 512], pap, [pt], [tGAF[dr]], eng="act")
            self.proj_fm(w_in, C_GAF + 16 * dr, 1, hr, HT, evg, mper=1, mrows=16)
        for hg in range(2):
            h0 = hg * 2
            pt0 = [T("gla_al%d" % hg, prev + self.arena_tiles)]
            GQ = AR[:, 0:2048].rearrange("p (h t) -> p h t", h=2)
            GK = AR[:, 2048:4096].rearrange("p (h t) -> p h t", h=2)
            GKT = AR[:, 4096:6144].rearrange("p (c d) -> p c d", c=8)
            GV = AR[:, 6144:10240].rearrange("p (c d) -> p c d", c=8)
            LSP = [AR[:, 10240 + d * 2048:10240 + (d + 1) * 2048].rearrange("p (c d) -> p c d", c=8) for d in range(2)]
            tGQ, tGK, tGKT, tGV = T("gq", pt0), T("gk", pt0), T("gkt", pt0), T("gv", pt0)
            tLSP = [T("lsp0", pt0), T("lsp1", pt0)]
            def evq(m, half, pap, pt):
                self.cp(GQ[:, m, half * 512:(half + 1) * 512], pap, [pt], [tGQ], eng="act")
            self.proj_fm(w_in, C_GQ + h0 * 128, 2, hr, HT, evq)

            def evk(m, half, pap, pt):
                self.cp(GK[:, m, half * 512:(half + 1) * 512], pap, [pt], [tGK], eng="dve")
            self.proj_fm(w_in, C_GK + h0 * 128, 2, hr, HT, evk)

            def evkt(tc, pap, pt):
                self.cp(GKT[:, tc, :], pap, [pt], [tGKT], eng="act")
            self.proj_tm(w_in, C_GK + h0 * 128, 256, hl, HT, evkt)
            for vv in range(2):
                def evv(tc, pap, pt, vv=vv):
                    self.cp(GV[:, tc, vv * 256:(vv + 1) * 256], pap, [pt], [tGV], eng="dve")
                self.proj_tm(w_in, C_GV + (h0 + vv) * 256, 256, hl, HT, evv)
            EX = f32_(AR, base, 512)
            tEX = T("ex", pt0)
            for dr in range(2):
                for tc in range(8):
                    b = self.psum()
                    pap = self.PS[:, b, 0:256]
                    self.mm(pap, [(GAF[0:17, dr, tc * 128:(tc + 1) * 128], GW[0:17, dr, h0 * 128:h0 * 128 + 256])],
                            [tGAF[dr], tGW], self.PT[b])
                    self.act(EX, pap, AF.Exp, [self.PT[b]], [tEX], scale=-1.0)
                    self.psfree(b)
                    self.act(LSP[dr][:, tc, :], EX, AF.Ln, [tEX], [tLSP[dr]], bias=1.0)
            S32 = f32_(AR, base + 512, 1024).rearrange("p (h d) -> p h d", h=2)
            SBF = AR[:, base + 1536:base + 2048].rearrange("p (h d) -> p h d", h=2)
            tS32, tSBF = T("s32", pt0), T("sbf", pt0)
            tEBL = T("ebl", pt0)
            tb = base + 2064
            ring = []
            for i in range(2):
                o = tb + i * 2560
                ring.append(dict(
                    QE=AR[:, o:o + 256].rearrange("p (h t) -> p h t", h=2), KE=AR[:, o + 256:o + 512].rearrange("p (h t) -> p h t", h=2),
                    KL=AR[:, o + 512:o + 768], AT=AR[:, o + 768:o + 1024].rearrange("p (h t) -> p h t", h=2),
                    EB=f32_(AR, o + 1024, 512), ENB=f32_(AR, o + 1536, 512), EKL=f32_(AR, o + 2048, 512),
                    EBL=f32_(AR, base + 2048 + 4 * i, 4), tEBL=T("ebl%d" % i, pt0),
                    t=[T("r%d_%d" % (i, q), pt0) for q in range(7)]))
            for dr in range(2):
                o_tf, o_t2, o_m = (B_TF, B_T2F, B_MF) if dr == 0 else (B_TB, B_T2B, B_MB)
                tri = CB[:, o_tf:o_tf + 128]
                tri2 = CB[:, o_t2:o_t2 + 128]
                mask = CB[:, o_m:o_m + 256].rearrange("p (h t) -> p h t", h=2)
                s0 = io["s0"][l, dr]
                self.dma("sp", S32[:, :, :], s0[:, h0:h0 + 2, :], [], [tS32])
                self.cp(SBF[:, :, :], S32[:, :, :], [tS32], [tSBF], eng="act")
                chunks = list(range(8)) if dr == 0 else list(range(7, -1, -1))
                lc = 127 if dr == 0 else 0

                def prep(ci, dr=dr, tri=tri, tri2=tri2, chunks=chunks, lc=lc):
                    c = chunks[ci]
                    R = ring[ci % 2]
                    tQE, tKE, tKL, tAT, tEB, tENB, tEKL = R["t"]
                    tsl = slice(c * 128, (c + 1) * 128)
                    b1 = self.psum()
                    for hh in range(2):
                        self.mm(self.PS[:, b1, hh * 128:(hh + 1) * 128], [(LSP[dr][:, c, hh * 128:(hh + 1) * 128], tri)],
                                [tLSP[dr], tCB], self.PT[b1])
                    self.act(R["EB"], self.PS[:, b1, 0:256], AF.Exp, [self.PT[b1]], [tEB])
                    self.act(R["ENB"], self.PS[:, b1, 0:256], AF.Exp, [self.PT[b1]], [tENB], scale=-1.0)
                    self.psfree(b1)
                    b2 = self.psum()
                    self.mm(self.PS[:, b2, 0:256], [(tri2, LSP[dr][:, c, :])], [tLSP[dr], tCB], self.PT[b2])
                    self.act(R["EKL"], self.PS[:, b2, 0:256], AF.Exp, [self.PT[b2]], [tEKL])
                    self.psfree(b2)
                    ebv = R["EB"].rearrange("p (h t) -> p h t", h=2)
                    enbv = R["ENB"].rearrange("p (h t) -> p h t", h=2)
                    self.stt(R["QE"], GQ[:, :, tsl], float(128 ** -0.5), ebv, ALU.mult, ALU.mult, [tGQ, tEB], [tQE])
                    self.tt(R["KE"], GK[:, :, tsl], enbv, ALU.mult, [tGK, tENB], [tKE])
                    self.tt(R["KL"], GKT[:, c, :], R["EKL"], ALU.mult, [tGKT, tEKL], [tKL])
                    self.cp(R["EBL"][:, 0:1], R["EB"][:, lc:lc + 1], [tEB], [R["tEBL"]], eng="dve")
                    self.cp(R["EBL"][:, 1:2], R["EB"][:, 128 + lc:128 + lc + 1], [tEB], [R["tEBL"]], eng="dve")

                def main(ci, dr=dr, mask=mask, chunks=chunks):
                    c = chunks[ci]
                    R = ring[ci % 2]
                    tQE, tKE, tKL, tAT, tEB, tENB, tEKL = R["t"]
                    EBL, tEBLr = R["EBL"], R["tEBL"]
                    tsl = slice(c * 128, (c + 1) * 128)
                    if ci > 0 and ci % 2 == 0:
                        self.ts(S32[:, :, :], S32[:, :, :], carry, None, ALU.mult, None, [tS32, tCS], [tS32])
                        self.cp(SBF[:, :, :], S32[:, :, :], [tS32], [tSBF], eng="act")
                    b3 = self.psum()
                    for hh in range(2):
                        self.mm(self.PS[:, b3, hh * 128:(hh + 1) * 128], [(R["KE"][:, hh, :], R["QE"][:, hh, :])],
                                [tKE, tQE], self.PT[b3])
                    self.tt(R["AT"], self.PS[:, b3, 0:256].rearrange("p (h t) -> p h t", h=2), mask, ALU.mult,
                            [self.PT[b3], tCB], [tAT])
                    self.psfree(b3)
                    b5 = self.psum()
                    for hh in range(2):
                        self.mm(self.PS[:, b5, hh * 256:(hh + 1) * 256], [(R["KL"][:, hh * 128:(hh + 1) * 128], GV[:, c, hh * 256:(hh + 1) * 256])],
                                [tKL, tGV], self.PT[b5])
                    b4 = self.psum()
                    for hh in range(2):
                        for e in range(2):
                            q = hh * 2 + e
                            self.mm(self.PS[:, b4, q * 128:(q + 1) * 128],
                                    [(SBF[:, hh, e * 128:(e + 1) * 128], R["QE"][:, hh, :]),
                                     (GV[:, c, hh * 256 + e * 128:hh * 256 + (e + 1) * 128], R["AT"][:, hh, :])],
                                    [tSBF, tQE, tGV, tAT], self.PT[b4])
                    for hh in range(2):
                        self.stt(S32[:, hh, :], S32[:, hh, :], EBL[:, hh:hh + 1], self.PS[:, b5, hh * 256:(hh + 1) * 256],
                                 ALU.mult, ALU.add, [tS32, tEBLr, self.PT[b5]], [tS32])
                    self.psfree(b5)
                    self.cp(SBF[:, :, :], S32[:, :, :], [tS32], [tSBF], eng="act")
                    ov = O[:, h0 * 2:h0 * 2 + 4, tsl]
                    pv = self.PS[:, b4, :].rearrange("p (q t) -> p q t", q=4)
                    tOs = tO[h0 * 2:h0 * 2 + 4]
                    if dr == 0:
                        self.cp(ov, pv, [self.PT[b4]], tOs, eng="act")
                    else:
                        self.tt(ov, pv, ov, ALU.add, [self.PT[b4]] + tOs, tOs)
                    self.psfree(b4)
                    if ci % 2 == 1:
                        seg = c // 2
                        dst = io["so"][dr][seg, l]
                        self.dma("sp", dst[h0:h0 + 2].rearrange("h p d -> p h d"), S32[:, :, :], [tS32], [])

                prep(0)
                for ci in range(8):
                    if ci + 1 < 8:
                        prep(ci + 1)
                    main(ci)
                    self.mod_tiles(l, 1, 1)
            self.arena_tiles = [tGQ, tGK, tGKT, tGV, tEX, tS32, tSBF, tEBL] + tLSP + [t for R in ring for t in R["t"]] + [R["tEBL"] for R in ring]
        pt1 = [T("gla_fin", self.arena_tiles)]
        OSQ = AR[:, 0:1024].rearrange("p (e t) -> p e t", e=2)
        tOSQ = T("osq", pt1)
        RSO = f32_(AR, 1024, 2048)
        tRSO = T("rso", pt1)
        SG = f32_(AR, 3072, 1024)
        tSG = T("sg", pt1)
        T3 = f32_(AR, 4096, 2048)
        tT3 = T("t3", pt1)
        for h in range(4):
            for half in range(2):
                hs = slice(half * 512, (half + 1) * 512)
                b = self.psum()
                for e in range(2):
                    self.act(OSQ[:, e, :], O[:, h * 2 + e, hs], AF.Square, [tO[h * 2 + e]], [tOSQ])
                self.mm(self.PS[:, b, :], [(ones, OSQ[:, 0, :]), (ones, OSQ[:, 1, :])], [tOSQ, tCB], self.PT[b])
                self.rstd(RSO[:, hs], self.PS[:, b, :], 256, [self.PT[b]], [tRSO])
                self.psfree(b)
            for e in range(2):
                c = h * 2 + e
                self.stt(T3, O[:, c, :], VEC[:, V_GN + e:V_GN + e + 1], RSO, ALU.mult, ALU.mult, [tO[c], tVEC, tRSO], [tT3])
                self.cp(O[:, c, :], T3, [tT3], [tO[c]], eng="act")
        def evr(m, half, pap, pt):
            hs = slice(half * 512, (half + 1) * 512)
            self.act(SG, pap, AF.Silu, [pt], [tSG])
            self.tt(O[:, m, hs], O[:, m, hs], SG, ALU.mult, [tO[m], tSG], [tO[m]])
        self.proj_fm(w_in, C_GR, 8, hr, HT, evr)
        self.arena_tiles = [tOSQ, tRSO, tSG, tT3]
        self.branch_out(l, io["w_gla"][l], lambda k, half: O[:, k, half * 512:(half + 1) * 512], tO, 2, hr, HT, first=True)
        self.arena_tiles = self.arena_tiles + tO + tGAF + [tGW]

    def branch_out(self, l, w_ap, rhs, rhs_tiles, gi, hr, HT, first=False):
        io, AR = self.io, self.A["AR"]
        w_in = io["w_in"][l]
        if first:
            self.tMGc = [T("mg%d" % m, self.arena_tiles) for m in range(KD)]
            self.tMG = T("mg_all")
        MG = AR[:, 0:16384].rearrange("p (k t) -> p k t", k=KD)
        self.MG = MG
        SGM = f32_(AR, 32768 - 4096, 2048).rearrange("p (a t) -> p a t", a=2)
        tSGM = [T("sgm0", self.arena_tiles), T("sgm1", self.arena_tiles)]
        for m2 in range(8):
            gv_, gt = self.wload(self.wsrc(w_in, C_GT + gi * D + m2 * 256, 256), KD, 256)
            bv_, bt = self.wload(self.wsrc(w_ap, m2 * 256, 256, 0, 8), 8, 256)
            for mi in range(2):
                m = m2 * 2 + mi
                for half in range(2):
                    hs = slice(half * 512, (half + 1) * 512)
                    b = self.psum()
                    self.mm(self.PS[:, b, :], [(gv_[:, k, mi * 128:(mi + 1) * 128], hr(k, half)) for k in range(KD)], [gt] + HT, self.PT[b])
                    self.act(SGM[:, half, :], self.PS[:, b, :], AF.Sigmoid, [self.PT[b]], [tSGM[half]])
                    self.psfree(b)
                    b = self.psum()
                    self.mm(self.PS[:, b, :], [(bv_[:, k, mi * 128:(mi + 1) * 128], rhs(k, half)) for k in range(8)], [bt] + rhs_tiles, self.PT[b])
                    if first:
                        self.tt(MG[:, m, hs], self.PS[:, b, :], SGM[:, half, :], ALU.mult, [self.PT[b], tSGM[half]], [self.tMGc[m], self.tMG])
                    else:
                        self.tt(SGM[:, half, :], self.PS[:, b, :], SGM[:, half, :], ALU.mult, [self.PT[b], tSGM[half]], [tSGM[half]])
                        self.tt(MG[:, m, hs], MG[:, m, hs], SGM[:, half, :], ALU.add, [self.tMGc[m], tSGM[half]], [self.tMGc[m], self.tMG])
                    self.psfree(b)
        self.arena_tiles = self.arena_tiles + tSGM

    def fnet(self, l, hr, hl, HT):
        io, A = self.io, self.A
        AR, CB = A["AR"], A["CB"]
        tCB = self.tCBx
        w_in = io["w_in"][l]
        prev = self.arena_tiles
        base = 16384
        FA = AR[:, base:base + 8192].rearrange("p (g t) -> p g t", g=8)
        tFA = [T("fa%d" % g, prev) for g in range(8)]
        FT = AR[:, base + 8192:base + 10240].rearrange("p (g t) -> p g t", g=2)
        tFT = [T("ft0", prev), T("ft1", prev)]
        XCS = AR[:, base + 10240:base + 14336].rearrange("p (c g d) -> p c g d", c=8, g=2)
        tXCS = T("xcs", prev)
        cs = CB[:, B_CS:B_CS + 256]
        for gp in range(4):
            def evf(m, half, pap, pt):
                self.cp(FT[:, m, half * 512:(half + 1) * 512], pap, [pt], [tFT[m]], eng="act")
            self.proj_fm(w_in, C_F + gp * 256, 2, hr, HT, evf)
            for g in range(2):
                for tc in range(8):
                    b = self.psum()
                    self.mm(self.PS[:, b, 0:256], [(FT[:, g, tc * 128:(tc + 1) * 128], cs)], [tFT[g], tCB], self.PT[b])
                    self.cp(XCS[:, tc, g, :], self.PS[:, b, 0:256], [self.PT[b]], [tXCS], eng="dve")
                    self.psfree(b)
            bs = [self.psum() for _ in range(4)]
            for tc in range(8):
                s = self.ws_rr
                self.ws_rr = (self.ws_rr + 1) % NWS
                cv = self.WS[s][:, 0:2048].rearrange("p (a t) -> p a t", a=2)
                self.dma("pool", cv, io["cts"][tc], [], [self.WT[s]])
                for g in range(2):
                    for half in range(2):
                        b = bs[g * 2 + half]
                        hs = slice(half * 512, (half + 1) * 512)
                        self.mm(self.PS[:, b, :], [(XCS[:, tc, g, 0:128], cv[:, 0, hs]), (XCS[:, tc, g, 128:256], cv[:, 1, hs])],
                                [tXCS, self.WT[s]], self.PT[b], start=(tc == 0), stop=(tc == 7))
            for g in range(2):
                for half in range(2):
                    b = bs[g * 2 + half]
                    self.cp(FA[:, gp * 2 + g, half * 512:(half + 1) * 512], self.PS[:, b, :], [self.PT[b]], [tFA[gp * 2 + g]],
                            eng=("act" if half == 0 else "dve"))
                    self.psfree(b)
        self.arena_tiles = tFT + [tXCS]
        self.branch_out(l, io["w_fourier"][l], lambda k, half: FA[:, k, half * 512:(half + 1) * 512], tFA, 0, hr, HT)
        self.arena_tiles = self.arena_tiles + tFA

    def attn(self, l, hr, hl, HT):
        io, A = self.io, self.A
        AR, CB, CS_, VEC = A["AR"], A["CB"], A["CST"], A["VEC"]
        tCB, tCS, tVEC = self.tCBx, self.tCSx, self.tVECx
        w_in = io["w_in"][l]
        prev = self.arena_tiles
        base = 16384
        ones = CB[:, B_ONE:B_ONE + 128]
        rm = CB[:, B_RM:B_RM + 128]
        QT = AR[:, base:base + 8192].rearrange("p (h t) -> p h t", h=8)
        tQT = [T("qt%d" % h, prev) for h in range(8)]
        KT = AR[:, base + 8192:base + 10752].rearrange("p (h s) -> p h s", h=2)
        tKT = [T("kt0", prev), T("kt1", prev)]
        VA = AR[:, base + 10752:base + 13312].rearrange("p (c d) -> p c d", c=10)
        tVA = T("va", prev)
        o = base + 13312
        ZSQ = AR[:, o:o + 512]
        RSQ = f32_(AR, o + 512, 1024)
        QN32 = f32_(AR, o + 1536, 1024)
        QNB = AR[:, o + 2560:o + 3072]
        TA = f32_(AR, o + 3072, 1024)
        LO = AR[:, o + 3072:o + 3584]
        identb = CB[:, B_ID:B_ID + 128]
        TB = f32_(AR, o + 4096, 1024)
        PTr = [AR[:, o + 5120 + i * 512:o + 5120 + (i + 1) * 512] for i in range(2)]
        tZSQ, tRSQ, tQN32, tQNB, tTA, tTB = [T(n, prev) for n in ("zsq", "rsq", "qn32", "qnb", "ta", "tb")]
        tPTr = [T("ptr0", prev), T("ptr1", prev)]
        KST = A["KST"]
        VST = A["VST"]
        tKST, tVST = self.tKST, self.tVST
        self.dma("pool", KT[:, :, 0:256], io["ctxk"][l], [], tKT)
        self.dma("pool", VA[:, 0:2, :], io["ctxv"][l], [], [tVA])
        if int(os.environ.get("KSUB", "9")) < 0:
            return
        def evv(tc, pap, pt):
            self.cp(VA[:, 2 + tc, :], pap, [pt], [tVA], eng="act")
            self.cp(VST[:, tc % 2, :], pap, [pt], [tVST[tc % 2]], eng="dve")
            self.dma("sp", io["vout"][l][tc * 128:(tc + 1) * 128, :], VST[:, tc % 2, :], [tVST[tc % 2]], [])
        self.proj_tm(w_in, C_V, 256, hl, HT, evv)
        SUB = int(os.environ.get("KSUB", "9"))
        if SUB < 1:
            return
        for hd in range(10):
            isk = hd >= 8
            col = (C_K + (hd - 8) * 128) if isk else (C_Q + hd * 128)
            gcol = V_KN if isk else V_QN

            def evq(m, half, pap, pt, hd=hd, isk=isk, gcol=gcol):
                hs = slice(half * 512, (half + 1) * 512)
                self.act(ZSQ, pap, AF.Square, [pt], [tZSQ])
                b = self.psum()
                self.mm(self.PS[:, b, :], [(ones, ZSQ)], [tZSQ, tCB], self.PT[b])
                self.rstd(RSQ, self.PS[:, b, :], 128, [self.PT[b]], [tRSQ])
                self.psfree(b)
                self.stt(QN32, pap, VEC[:, gcol:gcol + 1], RSQ, ALU.mult, ALU.mult, [pt, tVEC, tRSQ], [tQN32])
                self.cp(QNB, QN32, [tQN32], [tQNB], eng="act")
                if isk:
                    kv = hd - 8
                    ri = (kv * 2 + half) % 2
                    self.tt(LO, QN32, QNB, ALU.subtract, [tQN32, tQNB], [tTA])
                    for tq in range(4):
                        bb = self.psum()
                        self.mm(self.PS[:, bb, 0:128], [(QNB[:, tq * 128:(tq + 1) * 128], identb), (LO[:, tq * 128:(tq + 1) * 128], identb)],
                                [tQNB, tTA, tCB], self.PT[bb])
                        self.cp(KST[:, ri, tq, :], self.PS[:, bb, 0:128], [self.PT[bb]], [tKST[ri]], eng="dve")
                        self.psfree(bb)
                    self.dma("sp", io["kout"][l][half * 512:(half + 1) * 512, kv * 128:(kv + 1) * 128].rearrange("(q p) d -> p q d", p=128),
                             KST[:, ri, :, :], [tKST[ri]], [])
                b = self.psum()
                self.mm(self.PS[:, b, :], [(rm, QNB)], [tQNB, tCB], self.PT[b])
                self.tt(TA, QNB, CB[:, B_COS + half * 512:B_COS + (half + 1) * 512], ALU.mult, [tQNB, tCB], [tTA])
                self.tt(TB, self.PS[:, b, :], CB[:, B_SIN + half * 512:B_SIN + (half + 1) * 512], ALU.mult, [self.PT[b], tCB], [tTB])
                self.psfree(b)
                if isk:
                    dst, dt_ = KT[:, hd - 8, 256 + half * 512:256 + (half + 1) * 512], tKT[hd - 8]
                else:
                    dst, dt_ = QT[:, hd, hs], tQT[hd]
                self.tt(dst, TA, TB, ALU.add, [tTA, tTB], [dt_])
            self.proj_fm(w_in, col, 1, hr, HT, evq, mper=1)
        if SUB < 2:
            return
        RD = f32_(AR, o + 512, 1024)
        scale = float(128 ** -0.5)
        for kv in range(2):
            for jp in range(2):
                hh = kv * 4 + jp * 2
                for g in range(NSEG):
                    gs = slice(g * SEG, (g + 1) * SEG)
                    qv = QT[:, hh:hh + 2, gs]
                    bo = self.psum()
                    bd = self.psum()
                    bsn = {}

                    def S(kc):
                        bsn[kc] = self.psum()
                        self.mm(self.PS[:, bsn[kc], :].rearrange("p (h t) -> p h t", h=2), [(KT[:, kv, kc * 128:(kc + 1) * 128], qv)],
                                [tKT[kv], tQT[hh], tQT[hh + 1]], self.PT[bsn[kc]])
                    S(0)
                    for kc in range(10):
                        if kc + 1 < 10:
                            S(kc + 1)
                        i = kc % 2
                        b_ = bsn.pop(kc)
                        self.act(PTr[i], self.PS[:, b_, :], AF.Exp, [self.PT[b_], tCS], [tPTr[i]], scale=scale,
                                 bias=CS_[:, S_AB + kc * 4 + g:S_AB + kc * 4 + g + 1])
                        self.psfree(b_)
                        self.mm(self.PS[:, bo, :], [(VA[:, kc, kv * 128:(kv + 1) * 128], PTr[i])], [tVA, tPTr[i]], self.PT[bo],
                                start=(kc == 0), stop=(kc == 9))
                        self.mm(self.PS[:, bd, :], [(ones, PTr[i])], [tCB, tPTr[i]], self.PT[bd], start=(kc == 0), stop=(kc == 9))
                    self.mod_tiles(l + 1, 1, 0)
                    self.op("dve", lambda v, bd=bd: v.reciprocal(out=RD, in_=self.PS[:, bd, :]), [self.PT[bd]], [tRSQ])
                    self.psfree(bd)
                    self.tt(qv, self.PS[:, bo, :].rearrange("p (h t) -> p h t", h=2), RD.rearrange("p (h t) -> p h t", h=2), ALU.mult,
                            [self.PT[bo], tRSQ], [tQT[hh], tQT[hh + 1]])
                    self.psfree(bo)
        self.arena_tiles = tKT + [tVA, tZSQ, tRSQ, tQN32, tQNB, tTA, tTB] + tPTr
        if SUB < 3:
            return
        self.branch_out(l, io["w_attn"][l], lambda k, half: QT[:, k, half * 512:(half + 1) * 512], tQT, 1, hr, HT)
        self.arena_tiles = self.arena_tiles + tQT

    def ffn(self, l, hr, HT, X, XT, MOD, tMOD):
        io, A = self.io, self.A
        AR, CS_, VEC = A["AR"], A["CST"], A["VEC"]
        tCS, tVEC = self.tCSx, self.tVECx
        prev = self.arena_tiles
        carry = CS_[:, S_CARRY:S_CARRY + 1]
        ACT = AR[:, 0:22528].rearrange("p (j t) -> p j t", j=22)
        tACT = [T("act%d" % j, prev) for j in range(22)]
        o = 22528
        UP = [f32_(AR, o + s * 2080, 2064).rearrange("p (g t) -> p g t", g=4) for s in range(2)]
        tUP = [T("up0", prev), T("up1", prev)]
        AC = [f32_(AR, o + 4160 + s * 2048, 2048) for s in range(2)]
        tAC = [T("ac0", prev), T("ac1", prev)]
        SGT = f32_(AR, o + 8256, 2048)
        tSG = T("sgf", prev)
        for s in range(2):
            self.op("dve", lambda v, s=s: v.memset(UP[s][:, :, :], 0.0), [], [tUP[s]])
        w_up, w_dn = io["w_up"][l], io["w_down"][l]
        for sp in range(2):
            for jj in range(22):
                j = sp * 22 + jj
                wv, wt = self.wload(self.wsrc(w_up, j * 256, 256), KD, 256)
                for s in range(2):
                    for half in range(2):
                        b = self.psum()
                        self.mm(self.PS[:, b, :], [(wv[:, k, s * 128:(s + 1) * 128], hr(k, half)) for k in range(KD)], [wt] + HT, self.PT[b])
                        self.cp(UP[s][:, half * 2:half * 2 + 2, 1:257], self.PS[:, b, :].rearrange("p (g t) -> p g t", g=2),
                                [self.PT[b]], [tUP[s]], eng="act")
                        self.psfree(b)
                    self.ts(UP[s][:, 1:4, 0:1], UP[s][:, 0:3, 256:257], carry, None, ALU.mult, None, [tUP[s], tCS], [tUP[s]])
                    self.ts(UP[s][:, 0:3, 257:258], UP[s][:, 1:4, 1:2], carry, None, ALU.mult, None, [tUP[s], tCS], [tUP[s]])
                    cw = lambda t, s=s, j=j: VEC[:, V_CW + t * 88 + j * 2 + s:V_CW + t * 88 + j * 2 + s + 1]
                    acv = AC[s].rearrange("p (g t) -> p g t", g=4)
                    self.act(acv, UP[s][:, :, 1:257], AF.Identity, [tUP[s], tVEC], [tAC[s]], bias=VEC[:, V_CB + j * 2 + s:V_CB + j * 2 + s + 1],
                             scale=cw(1))
                    self.stt(acv, UP[s][:, :, 0:256], cw(0), acv, ALU.mult, ALU.add, [tUP[s], tVEC, tAC[s]], [tAC[s]])
                    self.stt(acv, UP[s][:, :, 2:258], cw(2), acv, ALU.mult, ALU.add, [tUP[s], tVEC, tAC[s]], [tAC[s]])
                self.act(SGT, AC[0], AF.Silu, [tAC[0]], [tSG])
                self.tt(ACT[:, jj, :], SGT, AC[1], ALU.mult, [tSG, tAC[1]], [tACT[jj]])
            for mp in range(8):
                bs = [self.psum() for _ in range(4)]
                for jh in range(2):
                    wv, wt = self.wload(self.wsrc(w_dn, mp * 256, 256, sp * 22 + jh * 11, 11), 11, 256)
                    for mi in range(2):
                        for half in range(2):
                            b = bs[mi * 2 + half]
                            self.mm(self.PS[:, b, :], [(wv[:, k, mi * 128:(mi + 1) * 128], ACT[:, jh * 11 + k, half * 512:(half + 1) * 512]) for k in range(11)],
                                    [wt] + tACT[jh * 11:jh * 11 + 11], self.PT[b], start=(jh == 0), stop=(jh == 1))
                for mi in range(2):
                    m = mp * 2 + mi
                    for half in range(2):
                        b = bs[mi * 2 + half]
                        xs = X[:, m, half * 512:(half + 1) * 512]
                        self.stt(xs, self.PS[:, b, :], MOD[:, 80 + m:81 + m], xs, ALU.mult, ALU.add, [self.PT[b], tMOD, XT[m]], [XT[m]])
                        self.psfree(b)
        self.arena_tiles = tACT + tUP + tAC + [tSG]

    def final(self, X, XT):
        io, A = self.io, self.A
        AR, CB, VEC = A["AR"], A["CB"], A["VEC"]
        prev = self.arena_tiles
        tar = T("fin", prev)
        SQ = [AR[:, i * 512:(i + 1) * 512] for i in range(4)]
        tSQ = [T("fsq%d" % i, [tar]) for i in range(4)]
        RS = AR[:, 2048:4096].bitcast(F32)
        tRS = T("frs", [tar])
        ones = CB[:, B_ONE:B_ONE + 128]
        for half in range(2):
            b = self.psum()
            for k in range(KD):
                i = k % 4
                self.act(SQ[i], X[:, k, half * 512:(half + 1) * 512], AF.Square, [XT[k]], [tSQ[i]])
                self.mm(self.PS[:, b, :], [(ones, SQ[i])], [tSQ[i], self.tCBx], self.PT[b], start=(k == 0), stop=(k == KD - 1))
            self.rstd(RS[:, half * 512:(half + 1) * 512], self.PS[:, b, :], D, [self.PT[b]], [tRS])
            self.psfree(b)
        for k in range(KD):
            self.stt(X[:, k, :], X[:, k, :], VEC[:, V_FN + k:V_FN + k + 1], RS, ALU.mult, ALU.mult, [XT[k], self.tVECx, tRS], [XT[k]])
        self.dma("sp", io["y"][:, :, :], X[:, :, :], XT, [])


def f32_(AR, off, n):
    return AR[:, off:off + n].bitcast(F32)


NAR = 35840


def build_program(depth):
    nc = bass.Bass("TRN2", target_bir_lowering=False)
    L = depth
    io = {}

    def din(name, shape):
        io[name] = nc.dram_tensor(name, list(shape), F32, kind="ExternalInput").ap()

    def dout(name, shape):
        io[name] = nc.dram_tensor(name, list(shape), F32, kind="ExternalOutput").ap()

    din("x", [128, KD, NT])
    din("cb16", [128, NB])
    din("cst", [128, NS])
    din("vecs", [L, 128, NV])
    din("gw", [L, 17, 2, 512])
    din("gones", [1, 2, NT])
    din("cts", [8, 128, 2, NT])
    din("ctxk", [L, 128, 2, 256])
    din("ctxv", [L, 128, 2, 256])
    din("s0", [L, 2, 128, 4, 256])
    din("w_ada", [L, D, 6 * D])
    din("w_in", [L, D, N_IN])
    din("w_fourier", [L, 1024, D])
    din("w_attn", [L, 1024, D])
    din("w_gla", [L, 1024, D])
    din("w_out", [L, D, D])
    din("w_up", [L, D, 2 * D_FF])
    din("w_down", [L, D_FF, D])
    dout("y", [128, KD, NT])
    dout("kout", [L, NT, 256])
    dout("vout", [L, NT, 256])
    dout("sof", [NSEG, L, 4, 128, 256])
    dout("sob", [NSEG, L, 4, 128, 256])
    io["so"] = [io["sof"], io["sob"]]

    k = K(nc, L)
    names = [("X", [128, KD, NT], F32), ("H", [128, KD, NT], BF16)] + [("W%d" % i, [128, WSLOT], BF16) for i in range(NWS)] + [
        ("CB", [128, NB], BF16), ("CST", [128, NS], F32), ("VEC", [128, NV], F32), ("MOD", [128, 128], F32),
        ("SCB", [128, 16], BF16), ("KST", [128, 2, 4, 128], F32), ("VST", [128, 2, 256], F32), ("AR", [128, NAR], BF16)]
    with ExitStack() as es:
        A = {}
        for n, shp, dt_ in names:
            A[n] = es.enter_context(nc.sbuf_tensor(n, shp, dt_))
        A["PS"] = es.enter_context(nc.psum_tensor("PS", [128, 8, 512], F32))
        sems = {e: es.enter_context(nc.semaphore("s_" + e)) for e in ("pe", "act", "dve", "pool", "sp")}
        dsems = [es.enter_context(nc.semaphore("d%d" % i)) for i in range(ND)]
        k.A = A
        k.build(io)
        block = es.enter_context(nc.Block())
        k.emit(block, sems, dsems)
    return nc


def _consts_bf(kind):
    cb = np.zeros((128, NB), np.float32)
    cb[:, B_ID:B_ID + 128] = np.eye(128)
    cb[:, B_ONE:B_ONE + 128] = 1.0
    s = np.arange(128)[:, None]
    t = np.arange(128)[None, :]
    mf = (s <= t).astype(np.float32)
    mb = (s >= t).astype(np.float32)
    cb[:, B_MF:B_MF + 512] = np.tile(mf, (1, 4))
    cb[:, B_MB:B_MB + 512] = np.tile(mb, (1, 4))
    cb[:, B_TF:B_TF + 128] = -mf / 16.0
    cb[:, B_TB:B_TB + 128] = -mb / 16.0
    cb[:, B_T2F:B_T2F + 128] = -(s > t).astype(np.float32) / 16.0
    cb[:, B_T2B:B_T2B + 128] = -(s < t).astype(np.float32) / 16.0
    dd = np.arange(128)
    partner = np.where((dd % 64) < 32, dd + 32, dd - 32)
    rm = np.zeros((128, 128), np.float32)
    rm[partner, dd] = 1.0
    cb[:, B_RM:B_RM + 128] = rm
    c = np.arange(128)[:, None] * np.arange(128)[None, :]
    ang = 2 * np.pi * (c % 128) / 128.0
    cb[:, B_CS:B_CS + 128] = np.cos(ang)
    cb[:, B_CS + 128:B_CS + 256] = np.sin(ang)
    if kind == "sample":
        tt = np.arange(NT)
        row = (tt // 64).astype(np.float64)
        colp = (tt % 64).astype(np.float64)
        inv = 10000.0 ** (-np.arange(0, 64, 2, dtype=np.float64) / 64.0)
        cosT = np.zeros((128, NT))
        sinT = np.zeros((128, NT))
        for d in range(128):
            axis_pos = row if d < 64 else colp
            i = (d % 64) % 32
            a = axis_pos * inv[i]
            cosT[d] = np.cos(a)
            sinT[d] = np.sin(a) * (-1.0 if (d % 64) < 32 else 1.0)
        cb[:, B_COS:B_COS + NT] = cosT
        cb[:, B_SIN:B_SIN + NT] = sinT
    else:
        cb[:, B_COS:B_COS + NT] = 1.0
        cb[:, B_SIN:B_SIN + NT] = 0.0
    return cb


def _cts(kind):
    T_ = NT if kind == "sample" else SEG
    t = np.arange(T_)
    ang = 2 * np.pi * ((t[:, None] * t[None, :]) % T_) / T_
    sc = 1.0 / np.sqrt(T_ * 128.0)
    cm, sm = np.cos(ang) * sc, -np.sin(ang) * sc
    if kind != "sample":
        z = np.zeros((NT, NT))
        z2 = np.zeros((NT, NT))
        for g in range(NSEG):
            z[g * SEG:(g + 1) * SEG, g * SEG:(g + 1) * SEG] = cm
            z2[g * SEG:(g + 1) * SEG, g * SEG:(g + 1) * SEG] = sm
        cm, sm = z, z2
    out = np.stack([cm.reshape(8, 128, NT), sm.reshape(8, 128, NT)], axis=2)
    return np.ascontiguousarray(out.astype(np.float32))


def _fm(v, n):
    return np.ascontiguousarray(np.asarray(v, np.float32).reshape(n, 128).T)


_PROG = {}
_SIM_HOOK = None


def kernel(x_prompt, x_sample, cache_k, cache_v, state_gla_fwd, state_gla_bwd, c, c_ctx,
           w_ada, b_ada, norm1, w_in, q_norm, k_norm, w_fourier, w_attn, w_gate_f, b_gate_f,
           w_gate_b, b_gate_b, gla_norm, w_gla, w_out, norm2, w_up, conv_w, conv_b, w_down,
           final_norm):
    L = int(np.asarray(w_ada).shape[0])
    f = lambda a: np.asarray(a, np.float32)
    x_prompt, x_sample, cache_k, cache_v = f(x_prompt), f(x_sample), f(cache_k), f(cache_v)
    sgf, sgb, c, c_ctx = f(state_gla_fwd), f(state_gla_bwd), f(c), f(c_ctx)
    vecs = np.zeros((L, 128, NV), np.float32)
    gw = np.zeros((L, 17, 2, 512), np.float32)
    cw = f(conv_w).reshape(L, 3, 2, NJ, 128)
    cbv = f(conv_b).reshape(L, 2, NJ, 128)
    for l in range(L):
        vecs[l, :, V_N1:V_N1 + 16] = _fm(norm1[l], 16)
        vecs[l, :, V_N2:V_N2 + 16] = _fm(norm2[l], 16)
        vecs[l, :, V_BADA:V_BADA + 96] = _fm(b_ada[l], 96)
        for t in range(3):
            vecs[l, :, V_CW + t * 88:V_CW + (t + 1) * 88] = cw[l, t].transpose(2, 1, 0).reshape(128, 88)
        vecs[l, :, V_CB:V_CB + 88] = cbv[l].transpose(2, 1, 0).reshape(128, 88)
        vecs[l, :, V_QN] = f(q_norm)[l]
        vecs[l, :, V_KN] = f(k_norm)[l]
        vecs[l, :, V_GN:V_GN + 2] = _fm(gla_norm[l], 2)
        vecs[l, :, V_FN:V_FN + 16] = _fm(final_norm, 16)
        gw[l, 0:16, 0] = f(w_gate_f)[l]
        gw[l, 0:16, 1] = f(w_gate_b)[l]
        gw[l, 16, 0] = f(b_gate_f)[l]
        gw[l, 16, 1] = f(b_gate_b)[l]
    w_up_r = np.ascontiguousarray(f(w_up).reshape(L, D, 2, NJ, 128).transpose(0, 1, 3, 2, 4).reshape(L, D, 2 * D_FF))
    shared = dict(vecs=vecs, gw=gw, gones=np.ones((1, 2, NT), np.float32), w_ada=f(w_ada), w_in=f(w_in),
                  w_fourier=f(w_fourier), w_attn=f(w_attn), w_gla=f(w_gla), w_out=f(w_out), w_up=w_up_r, w_down=f(w_down))
    cbs = {kd: _consts_bf(kd) for kd in ("prompt", "sample")}
    ctss = {kd: _cts(kd) for kd in ("prompt", "sample")}
    in_maps = []
    for ci in range(8):
        kind = "prompt" if ci < 4 else "sample"
        m = dict(shared)
        if kind == "prompt":
            xt = x_prompt[4 * ci:4 * ci + 4].reshape(NT, D)
            cv = c_ctx
        else:
            b = ci - 4
            xt = x_sample[b]
            cv = c[b]
        m["x"] = np.ascontiguousarray(xt.reshape(NT, KD, 128).transpose(2, 1, 0))
        m["cb16"] = cbs[kind]
        m["cts"] = ctss[kind]
        cst = np.zeros((128, NS), np.float32)
        cst[:, S_CARRY] = 1.0 if kind == "sample" else 0.0
        ab = np.zeros((10, 4), np.float32)
        if kind == "prompt":
            ab[:] = -30000.0
            for kc in range(2, 10):
                ab[kc, (kc - 2) // 2] = 0.0
        cst[:, S_AB:S_AB + 40] = ab.reshape(1, 40)
        cst[:, S_C:S_C + 16] = _fm(cv, 16)
        cst[:, S_ID:S_ID + 128] = np.eye(128, dtype=np.float32)
        m["cst"] = cst
        if kind == "sample":
            b = ci - 4
            m["ctxk"] = np.ascontiguousarray(cache_k[b].transpose(0, 3, 2, 1))
            m["ctxv"] = np.ascontiguousarray(cache_v[b].reshape(L, 2, 128, 256).transpose(0, 2, 1, 3))
            m["s0"] = np.ascontiguousarray(np.stack([sgf[b], sgb[b]], axis=1).transpose(0, 1, 3, 2, 4))
        else:
            m["ctxk"] = np.zeros((L, 128, 2, 256), np.float32)
            m["ctxv"] = np.zeros((L, 128, 2, 256), np.float32)
            m["s0"] = np.zeros((L, 2, 128, 4, 256), np.float32)
        in_maps.append(m)
    if _SIM_HOOK is not None:
        return _SIM_HOOK(in_maps, L)
    if L not in _PROG:
        _PROG[L] = build_program(L)
    res = run_bass_kernel_spmd(_PROG[L], in_maps, core_ids=list(range(8)))
    R = res.results
    return _gather(R, L)


def _gather(R, L):
    untr = lambda y: np.asarray(y).transpose(2, 1, 0).reshape(NT, D)
    y_prompt = np.stack([untr(R[ci]["y"]) for ci in range(4)]).reshape(16, SEG, D)
    y_sample = np.stack([untr(R[ci]["y"]) for ci in range(4, 8)])
    nk = np.concatenate([np.asarray(R[ci]["kout"]).reshape(L, 4, SEG, 2, 128).transpose(1, 0, 2, 3, 4) for ci in range(4)], axis=0)
    nv = np.concatenate([np.asarray(R[ci]["vout"]).reshape(L, 4, SEG, 2, 128).transpose(1, 0, 2, 3, 4) for ci in range(4)], axis=0)
    nsf = np.concatenate([np.asarray(R[ci]["sof"]) for ci in range(4)], axis=0)
    nsb = np.concatenate([np.asarray(R[ci]["sob"]) for ci in range(4)], axis=0)
    return (y_prompt.astype(np.float32), y_sample.astype(np.float32), np.ascontiguousarray(nk, np.float32),
            np.ascontiguousarray(nv, np.float32), np.ascontiguousarray(nsf, np.float32), np.ascontiguousarray(nsb, np.float32))
```

```python
import os
import numpy as np
import concourse.bass as bass
import concourse.mybir as mybir
from concourse.bass_utils import run_bass_kernel_spmd
from contextlib import ExitStack

F32 = mybir.dt.float32
BF16 = mybir.dt.bfloat16
AF = mybir.ActivationFunctionType
ALU = mybir.AluOpType

D = 2048
KD = 16
NT = 1024
NSEG = 4
SEG = 256
DEPTH = 4
D_FF = 5632
NJ = 44
N_IN = 11808
C_F, C_Q, C_K, C_V, C_GQ, C_GK, C_GV, C_GR, C_GAF, C_GAB, C_GT = (
    0, 1024, 2048, 2304, 2560, 3072, 3584, 4608, 5632, 5648, 5664)
EPS = 1e-6
WSLOT = 4096
NWS = 3
ND = 24

V_N1, V_N2, V_BADA, V_CW, V_CB, V_QN, V_KN, V_GN, V_FN = 0, 16, 32, 128, 392, 480, 481, 482, 484
NV = 500
B_ID, B_ONE, B_MF, B_MB, B_TF, B_TB, B_T2F, B_T2B, B_RM, B_CS, B_COS, B_SIN = (
    0, 128, 256, 768, 1280, 1408, 1536, 1664, 1792, 1920, 2176, 3200)
NB = 4224
S_CARRY, S_AB, S_C, S_ID = 0, 1, 41, 57
NS = 185


class T:
    __slots__ = ("name", "w", "r", "ps")

    def __init__(self, name, after=(), ps=False):
        self.name = name
        self.w = None
        self.r = []
        self.ps = ps
        for t in after:
            if t.w is not None:
                self.r.append(t.w)
            self.r.extend(t.r)


class Op:
    __slots__ = ("eng", "fn", "deps", "sig", "val", "dma", "dsem", "dval")


class K:
    def __init__(self, nc, depth):
        self.nc = nc
        self.L = depth
        self.ops = {e: [] for e in ("pe", "act", "dve", "pool", "sp")}
        self.dma_last = [None] * ND
        self.dma_cnt = [0] * ND
        self.dma_rr = {"sp": 0, "pool": ND // 2, "act": 0}
        self.ps_free = list(range(8))
        self.ws_rr = 0
        self.mod_state = {}

    def op(self, eng, fn, reads=(), writes=(), dma=False):
        o = Op()
        o.eng, o.fn, o.dma, o.sig, o.val = eng, fn, dma, False, 0
        deps = {}
        for t in reads:
            if t.w is not None:
                deps[id(t.w)] = t.w
            if t.ps:
                for r in t.r:
                    if r.eng != eng:
                        deps[id(r)] = r
        for t in writes:
            if t.w is not None:
                deps[id(t.w)] = t.w
            for r in t.r:
                deps[id(r)] = r
        if dma:
            h = ND // 2
            s = self.dma_rr[eng]
            lo = h if eng == "pool" else 0
            self.dma_rr[eng] = lo + (s - lo + 1) % h
            if self.dma_last[s] is not None:
                deps[id(self.dma_last[s])] = self.dma_last[s]
            self.dma_cnt[s] += 16
            o.dsem, o.dval = s, self.dma_cnt[s]
            self.dma_last[s] = o
        dl = []
        for d in deps.values():
            if (not d.dma) and d.eng == "pe" and eng == "pe" and not dma:
                continue
            dl.append(d)
            d.sig = True
        o.deps = dl
        for t in reads:
            if dma:
                t.r.append(o)
            else:
                t.r = [x for x in t.r if x.dma or x.eng != eng] + [o]
        for t in writes:
            t.w = o
            t.r = []
        self.ops[eng].append(o)
        return o

    def mod_tiles(self, l, n, part):
        if l >= self.L or l < 0:
            return
        key = (l, part)
        st = self.mod_state.get(key)
        if st is None:
            st = self.mod_state[key] = {"bank": self.psum(), "next": 0 if part == 0 else 16}
        if st["bank"] is None:
            return
        bm = st["bank"]
        end = 16 if part == 0 else 48
        w_ada = self.io["w_ada"][l]
        SCB = self.A["SCB"]
        for _ in range(n):
            j2 = st["next"]
            if j2 >= end:
                return
            st["next"] += 1
            wv, wt = self.wload(self.wsrc(w_ada, j2 * 256, 256), KD, 256)
            for mi in range(2):
                j = j2 * 2 + mi
                pairs = [(wv[:, k, mi * 128:(mi + 1) * 128], SCB[:, k:k + 1]) for k in range(KD)]
                self.mm(self.PS[:, bm, j:j + 1], pairs, [wt, self.tSCB], self.PT[bm])

    def mod_finish(self, l, part):
        self.mod_tiles(l, 48, part)
        st = self.mod_state[(l, part)]
        bm = st["bank"]
        MOD, VEC = self.A["MOD"], self.A["VEC"]
        tMOD, tVEC = self.tMODx, self.tVECx
        c0, c1 = (0, 32) if part == 0 else (32, 96)
        self.tt(MOD[:, c0:c1], self.PS[:, bm, c0:c1], VEC[:, V_BADA + c0:V_BADA + c1], ALU.add, [self.PT[bm], tVEC], [tMOD])
        self.psfree(bm)
        st["bank"] = None
        if part == 0:
            self.stt(MOD[:, 96:112], MOD[:, 16:32], 1.0, VEC[:, V_N1:V_N1 + 16], ALU.add, ALU.mult, [tMOD, tVEC], [tMOD])
        else:
            self.stt(MOD[:, 112:128], MOD[:, 64:80], 1.0, VEC[:, V_N2:V_N2 + 16], ALU.add, ALU.mult, [tMOD, tVEC], [tMOD])

    def psum(self, n=1):
        if n == 1:
            b = self.ps_free.pop(0)
            return b
        for i, b in enumerate(self.ps_free):
            if b % 2 == 0 and (b + 1) in self.ps_free:
                self.ps_free.remove(b)
                self.ps_free.remove(b + 1)
                return b
        raise RuntimeError("no psum pair")

    def psfree(self, b, n=1):
        for i in range(n):
            self.ps_free.append(b + i)

    def emit(self, block, sems, dsems):
        for e in self.ops:
            c = 0
            for o in self.ops[e]:
                if o.sig and not o.dma:
                    c += 1
                    o.val = c
        engs = {"pe": "tensor", "act": "scalar", "dve": "vector", "pool": "gpsimd", "sp": "sync"}
        for e, bn in engs.items():
            ops = self.ops[e]
            final = (e == "sp")

            def body(eng, ops=ops, e=e, final=final):
                waited = {}
                for o in ops:
                    for d in o.deps:
                        if d.dma:
                            key, sem, val = ("d", d.dsem), dsems[d.dsem], d.dval
                        else:
                            key, sem, val = ("e", d.eng), sems[d.eng], d.val
                        if waited.get(key, 0) < val:
                            eng.wait_ge(sem, val)
                            waited[key] = val
                    ins = o.fn(eng)
                    if o.dma:
                        ins.then_inc(dsems[o.dsem], 16)
                    elif o.sig:
                        ins.then_inc(sems[e], 1)
                if final:
                    for s in range(ND):
                        if self.dma_cnt[s] > 0:
                            eng.wait_ge(dsems[s], self.dma_cnt[s])
                    for en in ("pe", "act", "dve"):
                        n = sum(1 for o in self.ops[en] if o.sig)
                        if n:
                            eng.wait_ge(sems[en], n)

            getattr(block, bn)(body)

    def mm(self, out, pairs, reads, pst, start=True, stop=True):
        n = len(pairs)

        def fn(t):
            ins = None
            for i, (l, r) in enumerate(pairs):
                ins = t.matmul(out, lhsT=l, rhs=r, start=(start and i == 0), stop=(stop and i == n - 1))
            return ins
        return self.op("pe", fn, reads, [pst])

    def act(self, out, in_, func, reads, writes, bias=None, scale=None):
        kw = {}
        if bias is not None:
            kw["bias"] = bias
        if scale is not None:
            kw["scale"] = scale
        return self.op("act", lambda a: a.activation(out=out, in_=in_, func=func, **kw), reads, writes)

    def tt(self, out, a, b, op, reads, writes, eng="dve"):
        return self.op(eng, lambda v: v.tensor_tensor(out=out, in0=a, in1=b, op=op), reads, writes)

    def ts(self, out, a, s1, s2, op0, op1, reads, writes, eng="dve"):
        if op1 is None:
            return self.op(eng, lambda v: v.tensor_scalar(out=out, in0=a, scalar1=s1, scalar2=0.0, op0=op0, op1=ALU.add), reads, writes)
        return self.op(eng, lambda v: v.tensor_scalar(out=out, in0=a, scalar1=s1, scalar2=s2, op0=op0, op1=op1), reads, writes)

    def stt(self, out, a, s, b, op0, op1, reads, writes, eng="dve"):
        return self.op(eng, lambda v: v.scalar_tensor_tensor(out=out, in0=a, scalar=s, in1=b, op0=op0, op1=op1), reads, writes)

    def cp(self, out, in_, reads, writes, eng="dve"):
        if eng == "act":
            return self.op("act", lambda a: a.activation(out=out, in_=in_, func=AF.Copy), reads, writes)
        return self.op(eng, lambda v: v.tensor_copy(out=out, in_=in_), reads, writes)

    def dma(self, q, out, in_, reads, writes):
        return self.op(q, lambda e: e.dma_start(out=out, in_=in_), reads, writes, dma=True)

    def rstd(self, out, ps_ap, n, reads, writes):
        self.ts(out, ps_ap, 1.0 / n, EPS, ALU.mult, ALU.add, reads, writes)
        self.act(out, out, AF.Sqrt, writes, writes)
        self.op("dve", lambda v: v.reciprocal(out=out, in_=out), writes, writes)

    def wload(self, src_ap, kc, cols):
        s = self.ws_rr
        self.ws_rr = (self.ws_rr + 1) % NWS
        view = self.WS[s][:, 0:kc * cols].rearrange("p (k c) -> p k c", k=kc)
        self.dma("pool", view, src_ap, [], [self.WT[s]])
        return view, self.WT[s]

    def wsrc(self, w_ap, c0, cols, k0=0, kc=None):
        v = w_ap.rearrange("(k p) c -> p k c", p=128)
        if kc is None:
            return v[:, :, c0:c0 + cols]
        return v[:, k0:k0 + kc, c0:c0 + cols]

    def proj_fm(self, w_ap, c0, nm, rhs, rhs_tiles, evac, kc=KD, mper=2, mrows=128):
        m = 0
        while m < nm:
            g = min(mper, nm - m)
            wv, wt = self.wload(self.wsrc(w_ap, c0 + m * mrows, g * mrows, 0, kc), kc, g * mrows)
            for mi in range(g):
                for half in range(2):
                    b = self.psum()
                    pap = self.PS[0:mrows, b, :]
                    pairs = [(wv[:, k, mi * mrows:(mi + 1) * mrows], rhs(k, half)) for k in range(kc)]
                    self.mm(pap, pairs, [wt] + rhs_tiles, self.PT[b])
                    evac(m + mi, half, pap, self.PT[b])
                    self.psfree(b)
            m += g

    def proj_tm(self, w_ap, c0, cols, lhs, lhs_tiles, evac, kc=KD):
        wv, wt = self.wload(self.wsrc(w_ap, c0, cols, 0, kc), kc, cols)
        for tc in range(8):
            b = self.psum()
            pap = self.PS[:, b, 0:cols]
            pairs = [(lhs(k, tc), wv[:, k, :]) for k in range(kc)]
            self.mm(pap, pairs, [wt] + lhs_tiles, self.PT[b])
            evac(tc, pap, self.PT[b])
            self.psfree(b)

    def build(self, io):
        nc = self.nc
        L = self.L
        self.io = io
        A = self.A
        X, H = A["X"], A["H"]
        self.WS = [A["W%d" % i] for i in range(NWS)]
        self.WT = [T("W%d" % i) for i in range(NWS)]
        self.PS = A["PS"]
        self.PT = [T("PS%d" % i, ps=True) for i in range(8)]
        XT = [T("X%d" % k) for k in range(KD)]
        HT = [T("H%d" % k) for k in range(KD)]
        CB, CS_, VEC, MOD = A["CB"], A["CST"], A["VEC"], A["MOD"]
        tCB, tCS, tVEC, tMOD = T("CB"), T("CST"), T("VEC"), T("MOD")
        self.tCBx, self.tCSx, self.tVECx = tCB, tCS, tVEC
        self.tMODx = tMOD
        self.tKST, self.tVST = [T("kst0"), T("kst1")], [T("vst0"), T("vst1")]
        AR = A["AR"]
        self.arena_tiles = []

        def bf(off, n):
            return AR[:, off:off + n]

        def f32v(off, n):
            return AR[:, off:off + n].bitcast(F32)

        ident = CB[:, B_ID:B_ID + 128]
        ones = CB[:, B_ONE:B_ONE + 128]
        identf = CS_[:, S_ID:S_ID + 128]
        carry = CS_[:, S_CARRY:S_CARRY + 1]

        self.dma("sp", X[:, :, :], io["x"][:, :, :], [], XT)
        self.dma("pool", CB[:, :], io["cb16"][:, :], [], [tCB])
        self.dma("sp", CS_[:, :], io["cst"][:, :], [], [tCS])
        SCB = A["SCB"]
        tSCB = T("SCB")
        self.tSCB = tSCB
        self.act(SCB[:, :], CS_[:, S_C:S_C + 16], AF.Silu, [tCS], [tSCB])

        def hr(k, half):
            return H[:, k, half * 512:(half + 1) * 512]

        def hl(k, tc):
            return H[:, k, tc * 128:(tc + 1) * 128]

        STOP = int(os.environ.get("KSTOP", "99"))
        self.tMG = T("mg_dummy")
        for l in range(L):
            w_ada, w_in = io["w_ada"][l], io["w_in"][l]
            if STOP < 1:
                break
            self.dma("sp", VEC[:, :], io["vecs"][l], [], [tVEC])
            self.mod_finish(l, 0)

            if STOP < 2:
                break
            self.norm_mod(X, XT, H, HT, MOD, tMOD, 96, 0, T("ar_n1", self.arena_tiles))
            self.arena_tiles = []
            if STOP < 3:
                break
            self.gla(l, hr, hl, HT)
            self.mod_finish(l, 1)
            if STOP < 4:
                break
            self.fnet(l, hr, hl, HT)
            if STOP < 5:
                break
            self.attn(l, hr, hl, HT)
            if STOP < 6:
                break
            MG = self.MG

            def ev3(m, half, pap, pt):
                xs = X[:, m, half * 512:(half + 1) * 512]
                self.stt(xs, pap, MOD[:, 32 + m:33 + m], xs, ALU.mult, ALU.add, [pt, tMOD, XT[m]], [XT[m]])
            self.proj_fm(io["w_out"][l], 0, 16, lambda k, half: MG[:, k, half * 512:(half + 1) * 512], [self.tMG], ev3)
            if STOP < 7:
                break
            self.norm_mod(X, XT, H, HT, MOD, tMOD, 112, 48, T("ar_n2", self.arena_tiles + [self.tMG]))
            self.arena_tiles = []
            if STOP < 8:
                break
            self.ffn(l, hr, HT, X, XT, MOD, tMOD)

        self.final(X, XT)

    def norm_mod(self, X, XT, H, HT, MOD, tMOD, gcol, shcol, tar):
        AR = self.A["AR"]
        SQ = [AR[:, i * 512:(i + 1) * 512] for i in range(4)]
        tSQ = [T("sq%d" % i, [tar]) for i in range(4)]
        RS = AR[:, 2048:4096].bitcast(F32)
        tRS = T("rs", [tar])
        T32 = [AR[:, 4096 + i * 2048:4096 + (i + 1) * 2048].bitcast(F32) for i in range(2)]
        tT32 = [T("t32%d" % i, [tar]) for i in range(2)]
        ones = self.A["CB"][:, B_ONE:B_ONE + 128]
        for half in range(2):
            b = self.psum()
            for k in range(KD):
                i = k % 4
                self.act(SQ[i], X[:, k, half * 512:(half + 1) * 512], AF.Square, [XT[k]], [tSQ[i]])
                self.mm(self.PS[:, b, :], [(ones, SQ[i])], [tSQ[i], self.tCBx], self.PT[b], start=(k == 0), stop=(k == KD - 1))
            self.rstd(RS[:, half * 512:(half + 1) * 512], self.PS[:, b, :], D, [self.PT[b]], [tRS])
            self.psfree(b)
        for k in range(KD):
            i = k % 2
            self.stt(T32[i], X[:, k, :], MOD[:, gcol + k:gcol + k + 1], RS, ALU.mult, ALU.mult, [XT[k], tMOD, tRS], [tT32[i]])
            self.act(H[:, k, :], T32[i], AF.Identity, [tT32[i], tMOD], [HT[k]], bias=MOD[:, shcol + k:shcol + k + 1])
        self.arena_tiles = tSQ + [tRS] + tT32

    def gla(self, l, hr, hl, HT):
        io, A = self.io, self.A
        AR, CB, CS_, VEC, MOD = A["AR"], A["CB"], A["CST"], A["VEC"], A["MOD"]
        tCB, tCS, tVEC = self.tCBx, self.tCSx, self.tVECx
        w_in = io["w_in"][l]
        prev = self.arena_tiles
        ones = CB[:, B_ONE:B_ONE + 128]
        carry = CS_[:, S_CARRY:S_CARRY + 1]
        O = AR[:, 16384:24576].rearrange("p (c t) -> p c t", c=8)
        tO = [T("O%d" % c, prev) for c in range(8)]
        base = 24576
        GAF = AR[0:17, 32768:34816].rearrange("p (a t) -> p a t", a=2)
        GW = AR[0:17, 34816:35840].rearrange("p (a t) -> p a t", a=2)
        tGAF = [T("gaf0", prev), T("gaf1", prev)]
        tGW = T("gw", prev)
        self.dma("pool", GW[:, :, :], io["gw"][l], [], [tGW])
        self.dma("pool", AR[16:17, 32768:34816].rearrange("p (a t) -> p a t", a=2), io["gones"][:, :, :], [], tGAF)
        for dr in range(2):
            def evg(m, half, pap, pt, dr=dr):
                self.cp(GAF[0:16, dr, half * 512:(half + 1) * 512], pap, [pt], [tGAF[dr]], eng="act")
            self.proj_fm(w_in, C_GAF + 16 * dr, 1, hr, HT, evg, mper=1, mrows=16)
        for hg in range(2):
            h0 = hg * 2
            pt0 = [T("gla_al%d" % hg, prev + self.arena_tiles)]
            GQ = AR[:, 0:2048].rearrange("p (h t) -> p h t", h=2)
            GK = AR[:, 2048:4096].rearrange("p (h t) -> p h t", h=2)
            GKT = AR[:, 4096:6144].rearrange("p (c d) -> p c d", c=8)
            GV = AR[:, 6144:10240].rearrange("p (c d) -> p c d", c=8)
            LSP = [AR[:, 10240 + d * 2048:10240 + (d + 1) * 2048].rearrange("p (c d) -> p c d", c=8) for d in range(2)]
            tGQ, tGK, tGKT, tGV = T("gq", pt0), T("gk", pt0), T("gkt", pt0), T("gv", pt0)
            tLSP = [T("lsp0", pt0), T("lsp1", pt0)]
            def evq(m, half, pap, pt):
                self.cp(GQ[:, m, half * 512:(half + 1) * 512], pap, [pt], [tGQ], eng="act")
            self.proj_fm(w_in, C_GQ + h0 * 128, 2, hr, HT, evq)

            def evk(m, half, pap, pt):
                self.cp(GK[:, m, half * 512:(half + 1) * 512], pap, [pt], [tGK], eng="dve")
            self.proj_fm(w_in, C_GK + h0 * 128, 2, hr, HT, evk)

            def evkt(tc, pap, pt):
                self.cp(GKT[:, tc, :], pap, [pt], [tGKT], eng="act")
            self.proj_tm(w_in, C_GK + h0 * 128, 256, hl, HT, evkt)
            for vv in range(2):
                def evv(tc, pap, pt, vv=vv):
                    self.cp(GV[:, tc, vv * 256:(vv + 1) * 256], pap, [pt], [tGV], eng="dve")
                self.proj_tm(w_in, C_GV + (h0 + vv) * 256, 256, hl, HT, evv)
            EX = f32_(AR, base, 512)
            tEX = T("ex", pt0)
            for dr in range(2):
                for tc in range(8):
                    b = self.psum()
                    pap = self.PS[:, b, 0:256]
                    self.mm(pap, [(GAF[0:17, dr, tc * 128:(tc + 1) * 128], GW[0:17, dr, h0 * 128:h0 * 128 + 256])],
                            [tGAF[dr], tGW], self.PT[b])
                    self.act(EX, pap, AF.Exp, [self.PT[b]], [tEX], scale=-1.0)
                    self.psfree(b)
                    self.act(LSP[dr][:, tc, :], EX, AF.Ln, [tEX], [tLSP[dr]], bias=1.0)
            S32 = f32_(AR, base + 512, 1024).rearrange("p (h d) -> p h d", h=2)
            SBF = AR[:, base + 1536:base + 2048].rearrange("p (h d) -> p h d", h=2)
            tS32, tSBF = T("s32", pt0), T("sbf", pt0)
            tEBL = T("ebl", pt0)
            tb = base + 2064
            ring = []
            for i in range(2):
                o = tb + i * 2560
                ring.append(dict(
                    QE=AR[:, o:o + 256].rearrange("p (h t) -> p h t", h=2), KE=AR[:, o + 256:o + 512].rearrange("p (h t) -> p h t", h=2),
                    KL=AR[:, o + 512:o + 768], AT=AR[:, o + 768:o + 1024].rearrange("p (h t) -> p h t", h=2),
                    EB=f32_(AR, o + 1024, 512), ENB=f32_(AR, o + 1536, 512), EKL=f32_(AR, o + 2048, 512),
                    EBL=f32_(AR, base + 2048 + 4 * i, 4), tEBL=T("ebl%d" % i, pt0),
                    t=[T("r%d_%d" % (i, q), pt0) for q in range(7)]))
            for dr in range(2):
                o_tf, o_t2, o_m = (B_TF, B_T2F, B_MF) if dr == 0 else (B_TB, B_T2B, B_MB)
                tri = CB[:, o_tf:o_tf + 128]
                tri2 = CB[:, o_t2:o_t2 + 128]
                mask = CB[:, o_m:o_m + 256].rearrange("p (h t) -> p h t", h=2)
                s0 = io["s0"][l, dr]
                self.dma("sp", S32[:, :, :], s0[:, h0:h0 + 2, :], [], [tS32])
                self.cp(SBF[:, :, :], S32[:, :, :], [tS32], [tSBF], eng="act")
                chunks = list(range(8)) if dr == 0 else list(range(7, -1, -1))
                lc = 127 if dr == 0 else 0

                def prep(ci, dr=dr, tri=tri, tri2=tri2, chunks=chunks, lc=lc):
                    c = chunks[ci]
                    R = ring[ci % 2]
                    tQE, tKE, tKL, tAT, tEB, tENB, tEKL = R["t"]
                    tsl = slice(c * 128, (c + 1) * 128)
                    b1 = self.psum()
                    for hh in range(2):
                        self.mm(self.PS[:, b1, hh * 128:(hh + 1) * 128], [(LSP[dr][:, c, hh * 128:(hh + 1) * 128], tri)],
                                [tLSP[dr], tCB], self.PT[b1])
                    self.act(R["EB"], self.PS[:, b1, 0:256], AF.Exp, [self.PT[b1]], [tEB])
                    self.act(R["ENB"], self.PS[:, b1, 0:256], AF.Exp, [self.PT[b1]], [tENB], scale=-1.0)
                    self.psfree(b1)
                    b2 = self.psum()
                    self.mm(self.PS[:, b2, 0:256], [(tri2, LSP[dr][:, c, :])], [tLSP[dr], tCB], self.PT[b2])
                    self.act(R["EKL"], self.PS[:, b2, 0:256], AF.Exp, [self.PT[b2]], [tEKL])
                    self.psfree(b2)
                    ebv = R["EB"].rearrange("p (h t) -> p h t", h=2)
                    enbv = R["ENB"].rearrange("p (h t) -> p h t", h=2)
                    self.stt(R["QE"], GQ[:, :, tsl], float(128 ** -0.5), ebv, ALU.mult, ALU.mult, [tGQ, tEB], [tQE])
                    self.tt(R["KE"], GK[:, :, tsl], enbv, ALU.mult, [tGK, tENB], [tKE])
                    self.tt(R["KL"], GKT[:, c, :], R["EKL"], ALU.mult, [tGKT, tEKL], [tKL])
                    self.cp(R["EBL"][:, 0:1], R["EB"][:, lc:lc + 1], [tEB], [R["tEBL"]], eng="dve")
                    self.cp(R["EBL"][:, 1:2], R["EB"][:, 128 + lc:128 + lc + 1], [tEB], [R["tEBL"]], eng="dve")

                def main(ci, dr=dr, mask=mask, chunks=chunks):
                    c = chunks[ci]
                    R = ring[ci % 2]
                    tQE, tKE, tKL, tAT, tEB, tENB, tEKL = R["t"]
                    EBL, tEBLr = R["EBL"], R["tEBL"]
                    tsl = slice(c * 128, (c + 1) * 128)
                    if ci > 0 and ci % 2 == 0:
                        self.ts(S32[:, :, :], S32[:, :, :], carry, None, ALU.mult, None, [tS32, tCS], [tS32])
                        self.cp(SBF[:, :, :], S32[:, :, :], [tS32], [tSBF], eng="act")
                    b3 = self.psum()
                    for hh in range(2):
                        self.mm(self.PS[:, b3, hh * 128:(hh + 1) * 128], [(R["KE"][:, hh, :], R["QE"][:, hh, :])],
                                [tKE, tQE], self.PT[b3])
                    self.tt(R["AT"], self.PS[:, b3, 0:256].rearrange("p (h t) -> p h t", h=2), mask, ALU.mult,
                            [self.PT[b3], tCB], [tAT])
                    self.psfree(b3)
                    b5 = self.psum()
                    for hh in range(2):
                        self.mm(self.PS[:, b5, hh * 256:(hh + 1) * 256], [(R["KL"][:, hh * 128:(hh + 1) * 128], GV[:, c, hh * 256:(hh + 1) * 256])],
                                [tKL, tGV], self.PT[b5])
                    b4 = self.psum()
                    for hh in range(2):
                        for e in range(2):
                            q = hh * 2 + e
                            self.mm(self.PS[:, b4, q * 128:(q + 1) * 128],
                                    [(SBF[:, hh, e * 128:(e + 1) * 128], R["QE"][:, hh, :]),
                                     (GV[:, c, hh * 256 + e * 128:hh * 256 + (e + 1) * 128], R["AT"][:, hh, :])],
                                    [tSBF, tQE, tGV, tAT], self.PT[b4])
                    for hh in range(2):
                        self.stt(S32[:, hh, :], S32[:, hh, :], EBL[:, hh:hh + 1], self.PS[:, b5, hh * 256:(hh + 1) * 256],
                                 ALU.mult, ALU.add, [tS32, tEBLr, self.PT[b5]], [tS32])
                    self.psfree(b5)
                    self.cp(SBF[:, :, :], S32[:, :, :], [tS32], [tSBF], eng="act")
                    ov = O[:, h0 * 2:h0 * 2 + 4, tsl]
                    pv = self.PS[:, b4, :].rearrange("p (q t) -> p q t", q=4)
                    tOs = tO[h0 * 2:h0 * 2 + 4]
                    if dr == 0:
                        self.cp(ov, pv, [self.PT[b4]], tOs, eng="act")
                    else:
                        self.tt(ov, pv, ov, ALU.add, [self.PT[b4]] + tOs, tOs)
                    self.psfree(b4)
                    if ci % 2 == 1:
                        seg = c // 2
                        dst = io["so"][dr][seg, l]
                        self.dma("sp", dst[h0:h0 + 2].rearrange("h p d -> p h d"), S32[:, :, :], [tS32], [])

                prep(0)
                for ci in range(8):
                    if ci + 1 < 8:
                        prep(ci + 1)
                    main(ci)
                    if l == 0 or ci % 2 == 0:
                        self.mod_tiles(l, 1, 1)
            self.arena_tiles = [tGQ, tGK, tGKT, tGV, tEX, tS32, tSBF, tEBL] + tLSP + [t for R in ring for t in R["t"]] + [R["tEBL"] for R in ring]
        pt1 = [T("gla_fin", self.arena_tiles)]
        OSQ = AR[:, 0:1024].rearrange("p (e t) -> p e t", e=2)
        tOSQ = T("osq", pt1)
        RSO = f32_(AR, 1024, 2048)
        tRSO = T("rso", pt1)
        SG = f32_(AR, 3072, 1024)
        tSG = T("sg", pt1)
        T3 = f32_(AR, 4096, 2048)
        tT3 = T("t3", pt1)
        for h in range(4):
            for half in range(2):
                hs = slice(half * 512, (half + 1) * 512)
                b = self.psum()
                for e in range(2):
                    self.act(OSQ[:, e, :], O[:, h * 2 + e, hs], AF.Square, [tO[h * 2 + e]], [tOSQ])
                self.mm(self.PS[:, b, :], [(ones, OSQ[:, 0, :]), (ones, OSQ[:, 1, :])], [tOSQ, tCB], self.PT[b])
                self.rstd(RSO[:, hs], self.PS[:, b, :], 256, [self.PT[b]], [tRSO])
                self.psfree(b)
            for e in range(2):
                c = h * 2 + e
                self.stt(T3, O[:, c, :], VEC[:, V_GN + e:V_GN + e + 1], RSO, ALU.mult, ALU.mult, [tO[c], tVEC, tRSO], [tT3])
                self.cp(O[:, c, :], T3, [tT3], [tO[c]], eng="act")
        def evr(m, half, pap, pt):
            hs = slice(half * 512, (half + 1) * 512)
            self.act(SG, pap, AF.Silu, [pt], [tSG])
            self.tt(O[:, m, hs], O[:, m, hs], SG, ALU.mult, [tO[m], tSG], [tO[m]])
        self.proj_fm(w_in, C_GR, 8, hr, HT, evr)
        self.arena_tiles = [tOSQ, tRSO, tSG, tT3]
        self.branch_out(l, io["w_gla"][l], lambda k, half: O[:, k, half * 512:(half + 1) * 512], tO, 2, hr, HT, first=True)
        self.arena_tiles = self.arena_tiles + tO + tGAF + [tGW]

    def branch_out(self, l, w_ap, rhs, rhs_tiles, gi, hr, HT, first=False):
        io, AR = self.io, self.A["AR"]
        w_in = io["w_in"][l]
        if first:
            self.tMGc = [T("mg%d" % m, self.arena_tiles) for m in range(KD)]
            self.tMG = T("mg_all")
        MG = AR[:, 0:16384].rearrange("p (k t) -> p k t", k=KD)
        self.MG = MG
        SGM = f32_(AR, 32768 - 4096, 2048).rearrange("p (a t) -> p a t", a=2)
        tSGM = [T("sgm0", self.arena_tiles), T("sgm1", self.arena_tiles)]
        for m2 in range(8):
            gv_, gt = self.wload(self.wsrc(w_in, C_GT + gi * D + m2 * 256, 256), KD, 256)
            bv_, bt = self.wload(self.wsrc(w_ap, m2 * 256, 256, 0, 8), 8, 256)
            for mi in range(2):
                m = m2 * 2 + mi
                for half in range(2):
                    hs = slice(half * 512, (half + 1) * 512)
                    b = self.psum()
                    self.mm(self.PS[:, b, :], [(gv_[:, k, mi * 128:(mi + 1) * 128], hr(k, half)) for k in range(KD)], [gt] + HT, self.PT[b])
                    self.act(SGM[:, half, :], self.PS[:, b, :], AF.Sigmoid, [self.PT[b]], [tSGM[half]])
                    self.psfree(b)
                    b = self.psum()
                    self.mm(self.PS[:, b, :], [(bv_[:, k, mi * 128:(mi + 1) * 128], rhs(k, half)) for k in range(8)], [bt] + rhs_tiles, self.PT[b])
                    if first:
                        self.tt(MG[:, m, hs], self.PS[:, b, :], SGM[:, half, :], ALU.mult, [self.PT[b], tSGM[half]], [self.tMGc[m], self.tMG])
                    else:
                        self.tt(SGM[:, half, :], self.PS[:, b, :], SGM[:, half, :], ALU.mult, [self.PT[b], tSGM[half]], [tSGM[half]])
                        self.tt(MG[:, m, hs], MG[:, m, hs], SGM[:, half, :], ALU.add, [self.tMGc[m], tSGM[half]], [self.tMGc[m], self.tMG])
                    self.psfree(b)
        self.arena_tiles = self.arena_tiles + tSGM

    def fnet(self, l, hr, hl, HT):
        io, A = self.io, self.A
        AR, CB = A["AR"], A["CB"]
        tCB = self.tCBx
        w_in = io["w_in"][l]
        prev = self.arena_tiles
        base = 16384
        FA = AR[:, base:base + 8192].rearrange("p (g t) -> p g t", g=8)
        tFA = [T("fa%d" % g, prev) for g in range(8)]
        FT = AR[:, base + 8192:base + 10240].rearrange("p (g t) -> p g t", g=2)
        tFT = [T("ft0", prev), T("ft1", prev)]
        XCS = AR[:, base + 10240:base + 14336].rearrange("p (c g d) -> p c g d", c=8, g=2)
        tXCS = T("xcs", prev)
        cs = CB[:, B_CS:B_CS + 256]
        for gp in range(4):
            def evf(m, half, pap, pt):
                self.cp(FT[:, m, half * 512:(half + 1) * 512], pap, [pt], [tFT[m]], eng="act")
            self.proj_fm(w_in, C_F + gp * 256, 2, hr, HT, evf)
            for g in range(2):
                for tc in range(8):
                    b = self.psum()
                    self.mm(self.PS[:, b, 0:256], [(FT[:, g, tc * 128:(tc + 1) * 128], cs)], [tFT[g], tCB], self.PT[b])
                    self.cp(XCS[:, tc, g, :], self.PS[:, b, 0:256], [self.PT[b]], [tXCS], eng="dve")
                    self.psfree(b)
            bs = [self.psum() for _ in range(4)]
            for tc in range(8):
                s = self.ws_rr
                self.ws_rr = (self.ws_rr + 1) % NWS
                cv = self.WS[s][:, 0:2048].rearrange("p (a t) -> p a t", a=2)
                self.dma("pool", cv, io["cts"][tc], [], [self.WT[s]])
                for g in range(2):
                    for half in range(2):
                        b = bs[g * 2 + half]
                        hs = slice(half * 512, (half + 1) * 512)
                        self.mm(self.PS[:, b, :], [(XCS[:, tc, g, 0:128], cv[:, 0, hs]), (XCS[:, tc, g, 128:256], cv[:, 1, hs])],
                                [tXCS, self.WT[s]], self.PT[b], start=(tc == 0), stop=(tc == 7))
            for g in range(2):
                for half in range(2):
                    b = bs[g * 2 + half]
                    self.cp(FA[:, gp * 2 + g, half * 512:(half + 1) * 512], self.PS[:, b, :], [self.PT[b]], [tFA[gp * 2 + g]],
                            eng=("act" if half == 0 else "dve"))
                    self.psfree(b)
        self.arena_tiles = tFT + [tXCS]
        self.branch_out(l, io["w_fourier"][l], lambda k, half: FA[:, k, half * 512:(half + 1) * 512], tFA, 0, hr, HT)
        self.arena_tiles = self.arena_tiles + tFA

    def attn(self, l, hr, hl, HT):
        io, A = self.io, self.A
        AR, CB, CS_, VEC = A["AR"], A["CB"], A["CST"], A["VEC"]
        tCB, tCS, tVEC = self.tCBx, self.tCSx, self.tVECx
        w_in = io["w_in"][l]
        prev = self.arena_tiles
        base = 16384
        ones = CB[:, B_ONE:B_ONE + 128]
        rm = CB[:, B_RM:B_RM + 128]
        QT = AR[:, base:base + 8192].rearrange("p (h t) -> p h t", h=8)
        tQT = [T("qt%d" % h, prev) for h in range(8)]
        KT = AR[:, base + 8192:base + 10752].rearrange("p (h s) -> p h s", h=2)
        tKT = [T("kt0", prev), T("kt1", prev)]
        VA = AR[:, base + 10752:base + 13312].rearrange("p (c d) -> p c d", c=10)
        tVA = T("va", prev)
        o = base + 13312
        identb = CB[:, B_ID:B_ID + 128]
        qr = []
        for r in range(2):
            ob = o + r * 2560
            qr.append(dict(ZSQ=AR[:, ob:ob + 512], RSQ=f32_(AR, ob + 512, 1024), QNB=AR[:, ob + 1536:ob + 2048],
                           TA=AR[:, ob + 2048:ob + 2560], LO=AR[:, ob + 2048:ob + 2560],
                           tZSQ=T("zsq%d" % r, prev), tRSQ=T("rsq%d" % r, prev), tQNB=T("qnb%d" % r, prev), tTA=T("ta%d" % r, prev)))
        tRSQ = qr[0]["tRSQ"]
        PTr = [AR[:, o + 5120 + i * 512:o + 5120 + (i + 1) * 512] for i in range(2)]
        tPTr = [T("ptr0", prev), T("ptr1", prev)]
        KST = A["KST"]
        VST = A["VST"]
        tKST, tVST = self.tKST, self.tVST
        self.dma("pool", KT[:, :, 0:256], io["ctxk"][l], [], tKT)
        self.dma("pool", VA[:, 0:2, :], io["ctxv"][l], [], [tVA])
        if int(os.environ.get("KSUB", "9")) < 0:
            return
        def evv(tc, pap, pt):
            self.cp(VA[:, 2 + tc, :], pap, [pt], [tVA], eng="act")
            self.cp(VST[:, tc % 2, :], pap, [pt], [tVST[tc % 2]], eng="dve")
            self.dma("sp", io["vout"][l][tc * 128:(tc + 1) * 128, :], VST[:, tc % 2, :], [tVST[tc % 2]], [])
        self.proj_tm(w_in, C_V, 256, hl, HT, evv)
        SUB = int(os.environ.get("KSUB", "9"))
        if SUB < 1:
            return
        wts = {}

        def front(u):
            hd, half = u // 2, u % 2
            isk = hd >= 8
            R = qr[u % 2]
            col = (C_K + (hd - 8) * 128) if isk else (C_Q + hd * 128)
            gcol = V_KN if isk else V_QN
            if half == 0:
                wts[hd] = self.wload(self.wsrc(w_in, col, 128), KD, 128)
            wv, wt = wts[hd]
            bp = self.psum()
            pap = self.PS[:, bp, :]
            self.mm(pap, [(wv[:, k, :], hr(k, half)) for k in range(KD)], [wt] + HT, self.PT[bp])
            self.act(R["ZSQ"], pap, AF.Square, [self.PT[bp]], [R["tZSQ"]])
            b = self.psum()
            self.mm(self.PS[:, b, :], [(ones, R["ZSQ"])], [R["tZSQ"], tCB], self.PT[b])
            self.rstd(R["RSQ"], self.PS[:, b, :], 128, [self.PT[b]], [R["tRSQ"]])
            self.psfree(b)
            self.stt(pap, pap, VEC[:, gcol:gcol + 1], R["RSQ"], ALU.mult, ALU.mult, [self.PT[bp], tVEC, R["tRSQ"]], [self.PT[bp]])
            self.cp(R["QNB"], pap, [self.PT[bp]], [R["tQNB"]], eng="act")
            if isk:
                kv = hd - 8
                ri = (kv * 2 + half) % 2
                self.tt(R["LO"], pap, R["QNB"], ALU.subtract, [self.PT[bp], R["tQNB"]], [R["tTA"]])
                for tq in range(4):
                    bb = self.psum()
                    self.mm(self.PS[:, bb, 0:128], [(R["QNB"][:, tq * 128:(tq + 1) * 128], identb), (R["LO"][:, tq * 128:(tq + 1) * 128], identb)],
                            [R["tQNB"], R["tTA"], tCB], self.PT[bb])
                    self.cp(KST[:, ri, tq, :], self.PS[:, bb, 0:128], [self.PT[bb]], [tKST[ri]], eng="dve")
                    self.psfree(bb)
                self.dma("sp", io["kout"][l][half * 512:(half + 1) * 512, kv * 128:(kv + 1) * 128].rearrange("(q p) d -> p q d", p=128),
                         KST[:, ri, :, :], [tKST[ri]], [])
            self.psfree(bp)

        def back(u):
            hd, half = u // 2, u % 2
            isk = hd >= 8
            R = qr[u % 2]
            hs = slice(half * 512, (half + 1) * 512)
            b = self.psum()
            pb = self.PS[:, b, :]
            self.mm(pb, [(rm, R["QNB"])], [R["tQNB"], tCB], self.PT[b])
            self.tt(R["TA"], R["QNB"], CB[:, B_COS + half * 512:B_COS + (half + 1) * 512], ALU.mult, [R["tQNB"], tCB], [R["tTA"]])
            self.tt(pb, pb, CB[:, B_SIN + half * 512:B_SIN + (half + 1) * 512], ALU.mult, [self.PT[b], tCB], [self.PT[b]])
            if isk:
                dst, dt_ = KT[:, hd - 8, 256 + half * 512:256 + (half + 1) * 512], tKT[hd - 8]
            else:
                dst, dt_ = QT[:, hd, hs], tQT[hd]
            self.tt(dst, pb, R["TA"], ALU.add, [self.PT[b], R["tTA"]], [dt_])
            self.psfree(b)

        front(0)
        for u in range(20):
            if u + 1 < 20:
                front(u + 1)
            back(u)
        if SUB < 2:
            return
        RD = f32_(AR, o + 512, 1024)
        scale = float(128 ** -0.5)
        for kv in range(2):
            for jp in range(2):
                hh = kv * 4 + jp * 2
                for g in range(NSEG):
                    gs = slice(g * SEG, (g + 1) * SEG)
                    qv = QT[:, hh:hh + 2, gs]
                    bo = self.psum()
                    bd = self.psum()
                    bsn = {}

                    def S(kc):
                        bsn[kc] = self.psum()
                        self.mm(self.PS[:, bsn[kc], :].rearrange("p (h t) -> p h t", h=2), [(KT[:, kv, kc * 128:(kc + 1) * 128], qv)],
                                [tKT[kv], tQT[hh], tQT[hh + 1]], self.PT[bsn[kc]])
                    S(0)
                    for kc in range(10):
                        if kc + 1 < 10:
                            S(kc + 1)
                        i = kc % 2
                        b_ = bsn.pop(kc)
                        self.act(PTr[i], self.PS[:, b_, :], AF.Exp, [self.PT[b_], tCS], [tPTr[i]], scale=scale,
                                 bias=CS_[:, S_AB + kc * 4 + g:S_AB + kc * 4 + g + 1])
                        self.psfree(b_)
                        self.mm(self.PS[:, bo, :], [(VA[:, kc, kv * 128:(kv + 1) * 128], PTr[i])], [tVA, tPTr[i]], self.PT[bo],
                                start=(kc == 0), stop=(kc == 9))
                        self.mm(self.PS[:, bd, :], [(ones, PTr[i])], [tCB, tPTr[i]], self.PT[bd], start=(kc == 0), stop=(kc == 9))
                    self.mod_tiles(l + 1, 1, 0)
                    self.op("dve", lambda v, bd=bd: v.reciprocal(out=RD, in_=self.PS[:, bd, :]), [self.PT[bd]], [tRSQ])
                    self.psfree(bd)
                    self.tt(qv, self.PS[:, bo, :].rearrange("p (h t) -> p h t", h=2), RD.rearrange("p (h t) -> p h t", h=2), ALU.mult,
                            [self.PT[bo], tRSQ], [tQT[hh], tQT[hh + 1]])
                    self.psfree(bo)
        self.arena_tiles = tKT + [tVA] + [R[k] for R in qr for k in ("tZSQ", "tRSQ", "tQNB", "tTA")] + tPTr
        if SUB < 3:
            return
        self.branch_out(l, io["w_attn"][l], lambda k, half: QT[:, k, half * 512:(half + 1) * 512], tQT, 1, hr, HT)
        self.arena_tiles = self.arena_tiles + tQT

    def ffn(self, l, hr, HT, X, XT, MOD, tMOD):
        io, A = self.io, self.A
        AR, CS_, VEC = A["AR"], A["CST"], A["VEC"]
        tCS, tVEC = self.tCSx, self.tVECx
        prev = self.arena_tiles
        carry = CS_[:, S_CARRY:S_CARRY + 1]
        ACT = AR[:, 0:22528].rearrange("p (j t) -> p j t", j=22)
        tACT = [T("act%d" % j, prev) for j in range(22)]
        o = 22528
        UP = [f32_(AR, o + s * 2080, 2064).rearrange("p (g t) -> p g t", g=4) for s in range(2)]
        tUP = [T("up0", prev), T("up1", prev)]
        AC = [f32_(AR, o + 4160 + s * 2048, 2048) for s in range(2)]
        tAC = [T("ac0", prev), T("ac1", prev)]
        SGT = f32_(AR, o + 8256, 2048)
        tSG = T("sgf", prev)
        for s in range(2):
            self.op("dve", lambda v, s=s: v.memset(UP[s][:, :, :], 0.0), [], [tUP[s]])
        w_up, w_dn = io["w_up"][l], io["w_down"][l]
        for sp in range(2):
            for jj in range(22):
                j = sp * 22 + jj
                wv, wt = self.wload(self.wsrc(w_up, j * 256, 256), KD, 256)
                for s in range(2):
                    for half in range(2):
                        b = self.psum()
                        self.mm(self.PS[:, b, :], [(wv[:, k, s * 128:(s + 1) * 128], hr(k, half)) for k in range(KD)], [wt] + HT, self.PT[b])
                        self.cp(UP[s][:, half * 2:half * 2 + 2, 1:257], self.PS[:, b, :].rearrange("p (g t) -> p g t", g=2),
                                [self.PT[b]], [tUP[s]], eng="act")
                        self.psfree(b)
                    self.ts(UP[s][:, 1:4, 0:1], UP[s][:, 0:3, 256:257], carry, None, ALU.mult, None, [tUP[s], tCS], [tUP[s]])
                    self.ts(UP[s][:, 0:3, 257:258], UP[s][:, 1:4, 1:2], carry, None, ALU.mult, None, [tUP[s], tCS], [tUP[s]])
                    cw = lambda t, s=s, j=j: VEC[:, V_CW + t * 88 + j * 2 + s:V_CW + t * 88 + j * 2 + s + 1]
                    acv = AC[s].rearrange("p (g t) -> p g t", g=4)
                    self.act(acv, UP[s][:, :, 1:257], AF.Identity, [tUP[s], tVEC], [tAC[s]], bias=VEC[:, V_CB + j * 2 + s:V_CB + j * 2 + s + 1],
                             scale=cw(1))
                    self.stt(acv, UP[s][:, :, 0:256], cw(0), acv, ALU.mult, ALU.add, [tUP[s], tVEC, tAC[s]], [tAC[s]])
                    self.stt(acv, UP[s][:, :, 2:258], cw(2), acv, ALU.mult, ALU.add, [tUP[s], tVEC, tAC[s]], [tAC[s]])
                self.act(SGT, AC[0], AF.Silu, [tAC[0]], [tSG])
                self.tt(ACT[:, jj, :], SGT, AC[1], ALU.mult, [tSG, tAC[1]], [tACT[jj]])
                if j % 3 == 0:
                    self.mod_tiles(l + 1, 1, 1)
            for mp in range(8):
                bs = [self.psum() for _ in range(4)]
                for jh in range(2):
                    wv, wt = self.wload(self.wsrc(w_dn, mp * 256, 256, sp * 22 + jh * 11, 11), 11, 256)
                    for mi in range(2):
                        for half in range(2):
                            b = bs[mi * 2 + half]
                            self.mm(self.PS[:, b, :], [(wv[:, k, mi * 128:(mi + 1) * 128], ACT[:, jh * 11 + k, half * 512:(half + 1) * 512]) for k in range(11)],
                                    [wt] + tACT[jh * 11:jh * 11 + 11], self.PT[b], start=(jh == 0), stop=(jh == 1))
                for mi in range(2):
                    m = mp * 2 + mi
                    for half in range(2):
                        b = bs[mi * 2 + half]
                        xs = X[:, m, half * 512:(half + 1) * 512]
                        self.stt(xs, self.PS[:, b, :], MOD[:, 80 + m:81 + m], xs, ALU.mult, ALU.add, [self.PT[b], tMOD, XT[m]], [XT[m]])
                        self.psfree(b)
        self.arena_tiles = tACT + tUP + tAC + [tSG]

    def final(self, X, XT):
        io, A = self.io, self.A
        AR, CB, VEC = A["AR"], A["CB"], A["VEC"]
        prev = self.arena_tiles
        tar = T("fin", prev)
        SQ = [AR[:, i * 512:(i + 1) * 512] for i in range(4)]
        tSQ = [T("fsq%d" % i, [tar]) for i in range(4)]
        RS = AR[:, 2048:4096].bitcast(F32)
        tRS = T("frs", [tar])
        ones = CB[:, B_ONE:B_ONE + 128]
        for half in range(2):
            b = self.psum()
            for k in range(KD):
                i = k % 4
                self.act(SQ[i], X[:, k, half * 512:(half + 1) * 512], AF.Square, [XT[k]], [tSQ[i]])
                self.mm(self.PS[:, b, :], [(ones, SQ[i])], [tSQ[i], self.tCBx], self.PT[b], start=(k == 0), stop=(k == KD - 1))
            self.rstd(RS[:, half * 512:(half + 1) * 512], self.PS[:, b, :], D, [self.PT[b]], [tRS])
            self.psfree(b)
        for k in range(KD):
            self.stt(X[:, k, :], X[:, k, :], VEC[:, V_FN + k:V_FN + k + 1], RS, ALU.mult, ALU.mult, [XT[k], self.tVECx, tRS], [XT[k]])
        self.dma("sp", io["y"][:, :, :], X[:, :, :], XT, [])


def f32_(AR, off, n):
    return AR[:, off:off + n].bitcast(F32)


NAR = 35840


def build_program(depth):
    nc = bass.Bass("TRN2", target_bir_lowering=False)
    L = depth
    io = {}

    def din(name, shape):
        io[name] = nc.dram_tensor(name, list(shape), F32, kind="ExternalInput").ap()

    def dout(name, shape):
        io[name] = nc.dram_tensor(name, list(shape), F32, kind="ExternalOutput").ap()

    din("x", [128, KD, NT])
    din("cb16", [128, NB])
    din("cst", [128, NS])
    din("vecs", [L, 128, NV])
    din("gw", [L, 17, 2, 512])
    din("gones", [1, 2, NT])
    din("cts", [8, 128, 2, NT])
    din("ctxk", [L, 128, 2, 256])
    din("ctxv", [L, 128, 2, 256])
    din("s0", [L, 2, 128, 4, 256])
    din("w_ada", [L, D, 6 * D])
    din("w_in", [L, D, N_IN])
    din("w_fourier", [L, 1024, D])
    din("w_attn", [L, 1024, D])
    din("w_gla", [L, 1024, D])
    din("w_out", [L, D, D])
    din("w_up", [L, D, 2 * D_FF])
    din("w_down", [L, D_FF, D])
    dout("y", [128, KD, NT])
    dout("kout", [L, NT, 256])
    dout("vout", [L, NT, 256])
    dout("sof", [NSEG, L, 4, 128, 256])
    dout("sob", [NSEG, L, 4, 128, 256])
    io["so"] = [io["sof"], io["sob"]]

    k = K(nc, L)
    names = [("X", [128, KD, NT], F32), ("H", [128, KD, NT], BF16)] + [("W%d" % i, [128, WSLOT], BF16) for i in range(NWS)] + [
        ("CB", [128, NB], BF16), ("CST", [128, NS], F32), ("VEC", [128, NV], F32), ("MOD", [128, 128], F32),
        ("SCB", [128, 16], BF16), ("KST", [128, 2, 4, 128], F32), ("VST", [128, 2, 256], F32), ("AR", [128, NAR], BF16)]
    with ExitStack() as es:
        A = {}
        for n, shp, dt_ in names:
            A[n] = es.enter_context(nc.sbuf_tensor(n, shp, dt_))
        A["PS"] = es.enter_context(nc.psum_tensor("PS", [128, 8, 512], F32))
        sems = {e: es.enter_context(nc.semaphore("s_" + e)) for e in ("pe", "act", "dve", "pool", "sp")}
        dsems = [es.enter_context(nc.semaphore("d%d" % i)) for i in range(ND)]
        k.A = A
        k.build(io)
        block = es.enter_context(nc.Block())
        k.emit(block, sems, dsems)
    return nc


def _consts_bf(kind):
    cb = np.zeros((128, NB), np.float32)
    cb[:, B_ID:B_ID + 128] = np.eye(128)
    cb[:, B_ONE:B_ONE + 128] = 1.0
    s = np.arange(128)[:, None]
    t = np.arange(128)[None, :]
    mf = (s <= t).astype(np.float32)
    mb = (s >= t).astype(np.float32)
    cb[:, B_MF:B_MF + 512] = np.tile(mf, (1, 4))
    cb[:, B_MB:B_MB + 512] = np.tile(mb, (1, 4))
    cb[:, B_TF:B_TF + 128] = -mf / 16.0
    cb[:, B_TB:B_TB + 128] = -mb / 16.0
    cb[:, B_T2F:B_T2F + 128] = -(s > t).astype(np.float32) / 16.0
    cb[:, B_T2B:B_T2B + 128] = -(s < t).astype(np.float32) / 16.0
    dd = np.arange(128)
    partner = np.where((dd % 64) < 32, dd + 32, dd - 32)
    rm = np.zeros((128, 128), np.float32)
    rm[partner, dd] = 1.0
    cb[:, B_RM:B_RM + 128] = rm
    c = np.arange(128)[:, None] * np.arange(128)[None, :]
    ang = 2 * np.pi * (c % 128) / 128.0
    cb[:, B_CS:B_CS + 128] = np.cos(ang)
    cb[:, B_CS + 128:B_CS + 256] = np.sin(ang)
    if kind == "sample":
        tt = np.arange(NT)
        row = (tt // 64).astype(np.float64)
        colp = (tt % 64).astype(np.float64)
        inv = 10000.0 ** (-np.arange(0, 64, 2, dtype=np.float64) / 64.0)
        cosT = np.zeros((128, NT))
        sinT = np.zeros((128, NT))
        for d in range(128):
            axis_pos = row if d < 64 else colp
            i = (d % 64) % 32
            a = axis_pos * inv[i]
            cosT[d] = np.cos(a)
            sinT[d] = np.sin(a) * (-1.0 if (d % 64) < 32 else 1.0)
        cb[:, B_COS:B_COS + NT] = cosT
        cb[:, B_SIN:B_SIN + NT] = sinT
    else:
        cb[:, B_COS:B_COS + NT] = 1.0
        cb[:, B_SIN:B_SIN + NT] = 0.0
    return cb


def _cts(kind):
    T_ = NT if kind == "sample" else SEG
    t = np.arange(T_)
    ang = 2 * np.pi * ((t[:, None] * t[None, :]) % T_) / T_
    sc = 1.0 / np.sqrt(T_ * 128.0)
    cm, sm = np.cos(ang) * sc, -np.sin(ang) * sc
    if kind != "sample":
        z = np.zeros((NT, NT))
        z2 = np.zeros((NT, NT))
        for g in range(NSEG):
            z[g * SEG:(g + 1) * SEG, g * SEG:(g + 1) * SEG] = cm
            z2[g * SEG:(g + 1) * SEG, g * SEG:(g + 1) * SEG] = sm
        cm, sm = z, z2
    out = np.stack([cm.reshape(8, 128, NT), sm.reshape(8, 128, NT)], axis=2)
    return np.ascontiguousarray(out.astype(np.float32))


def _fm(v, n):
    return np.ascontiguousarray(np.asarray(v, np.float32).reshape(n, 128).T)


_PROG = {}
_SIM_HOOK = None


def kernel(x_prompt, x_sample, cache_k, cache_v, state_gla_fwd, state_gla_bwd, c, c_ctx,
           w_ada, b_ada, norm1, w_in, q_norm, k_norm, w_fourier, w_attn, w_gate_f, b_gate_f,
           w_gate_b, b_gate_b, gla_norm, w_gla, w_out, norm2, w_up, conv_w, conv_b, w_down,
           final_norm):
    L = int(np.asarray(w_ada).shape[0])
    f = lambda a: np.asarray(a, np.float32)
    x_prompt, x_sample, cache_k, cache_v = f(x_prompt), f(x_sample), f(cache_k), f(cache_v)
    sgf, sgb, c, c_ctx = f(state_gla_fwd), f(state_gla_bwd), f(c), f(c_ctx)
    vecs = np.zeros((L, 128, NV), np.float32)
    gw = np.zeros((L, 17, 2, 512), np.float32)
    cw = f(conv_w).reshape(L, 3, 2, NJ, 128)
    cbv = f(conv_b).reshape(L, 2, NJ, 128)
    for l in range(L):
        vecs[l, :, V_N1:V_N1 + 16] = _fm(norm1[l], 16)
        vecs[l, :, V_N2:V_N2 + 16] = _fm(norm2[l], 16)
        vecs[l, :, V_BADA:V_BADA + 96] = _fm(b_ada[l], 96)
        for t in range(3):
            vecs[l, :, V_CW + t * 88:V_CW + (t + 1) * 88] = cw[l, t].transpose(2, 1, 0).reshape(128, 88)
        vecs[l, :, V_CB:V_CB + 88] = cbv[l].transpose(2, 1, 0).reshape(128, 88)
        vecs[l, :, V_QN] = f(q_norm)[l]
        vecs[l, :, V_KN] = f(k_norm)[l]
        vecs[l, :, V_GN:V_GN + 2] = _fm(gla_norm[l], 2)
        vecs[l, :, V_FN:V_FN + 16] = _fm(final_norm, 16)
        gw[l, 0:16, 0] = f(w_gate_f)[l]
        gw[l, 0:16, 1] = f(w_gate_b)[l]
        gw[l, 16, 0] = f(b_gate_f)[l]
        gw[l, 16, 1] = f(b_gate_b)[l]
    w_up_r = np.ascontiguousarray(f(w_up).reshape(L, D, 2, NJ, 128).transpose(0, 1, 3, 2, 4).reshape(L, D, 2 * D_FF))
    shared = dict(vecs=vecs, gw=gw, gones=np.ones((1, 2, NT), np.float32), w_ada=f(w_ada), w_in=f(w_in),
                  w_fourier=f(w_fourier), w_attn=f(w_attn), w_gla=f(w_gla), w_out=f(w_out), w_up=w_up_r, w_down=f(w_down))
    cbs = {kd: _consts_bf(kd) for kd in ("prompt", "sample")}
    ctss = {kd: _cts(kd) for kd in ("prompt", "sample")}
    in_maps = []
    for ci in range(8):
        kind = "prompt" if ci < 4 else "sample"
        m = dict(shared)
        if kind == "prompt":
            xt = x_prompt[4 * ci:4 * ci + 4].reshape(NT, D)
            cv = c_ctx
        else:
            b = ci - 4
            xt = x_sample[b]
            cv = c[b]
        m["x"] = np.ascontiguousarray(xt.reshape(NT, KD, 128).transpose(2, 1, 0))
        m["cb16"] = cbs[kind]
        m["cts"] = ctss[kind]
        cst = np.zeros((128, NS), np.float32)
        cst[:, S_CARRY] = 1.0 if kind == "sample" else 0.0
        ab = np.zeros((10, 4), np.float32)
        if kind == "prompt":
            ab[:] = -30000.0
            for kc in range(2, 10):
                ab[kc, (kc - 2) // 2] = 0.0
        cst[:, S_AB:S_AB + 40] = ab.reshape(1, 40)
        cst[:, S_C:S_C + 16] = _fm(cv, 16)
        cst[:, S_ID:S_ID + 128] = np.eye(128, dtype=np.float32)
        m["cst"] = cst
        if kind == "sample":
            b = ci - 4
            m["ctxk"] = np.ascontiguousarray(cache_k[b].transpose(0, 3, 2, 1))
            m["ctxv"] = np.ascontiguousarray(cache_v[b].reshape(L, 2, 128, 256).transpose(0, 2, 1, 3))
            m["s0"] = np.ascontiguousarray(np.stack([sgf[b], sgb[b]], axis=1).transpose(0, 1, 3, 2, 4))
        else:
            m["ctxk"] = np.zeros((L, 128, 2, 256), np.float32)
            m["ctxv"] = np.zeros((L, 128, 2, 256), np.float32)
            m["s0"] = np.zeros((L, 2, 128, 4, 256), np.float32)
        in_maps.append(m)
    if _SIM_HOOK is not None:
        return _SIM_HOOK(in_maps, L)
    if L not in _PROG:
        _PROG[L] = build_program(L)
    res = run_bass_kernel_spmd(_PROG[L], in_maps, core_ids=list(range(8)))
    R = res.results
    return _gather(R, L)


def _gather(R, L):
    untr = lambda y: np.asarray(y).transpose(2, 1, 0).reshape(NT, D)
    y_prompt = np.stack([untr(R[ci]["y"]) for ci in range(4)]).reshape(16, SEG, D)
    y_sample = np.stack([untr(R[ci]["y"]) for ci in range(4, 8)])
    nk = np.concatenate([np.asarray(R[ci]["kout"]).reshape(L, 4, SEG, 2, 128).transpose(1, 0, 2, 3, 4) for ci in range(4)], axis=0)
    nv = np.concatenate([np.asarray(R[ci]["vout"]).reshape(L, 4, SEG, 2, 128).transpose(1, 0, 2, 3, 4) for ci in range(4)], axis=0)
    nsf = np.concatenate([np.asarray(R[ci]["sof"]) for ci in range(4)], axis=0)
    nsb = np.concatenate([np.asarray(R[ci]["sob"]) for ci in range(4)], axis=0)
    return (y_prompt.astype(np.float32), y_sample.astype(np.float32), np.ascontiguousarray(nk, np.float32),
            np.ascontiguousarray(nv, np.float32), np.ascontiguousarray(nsf, np.float32), np.ascontiguousarray(nsb, np.float32))
```

```python
import os
import numpy as np
import concourse.bass as bass
import concourse.mybir as mybir
from concourse.bass_utils import run_bass_kernel_spmd
from contextlib import ExitStack

F32 = mybir.dt.float32
BF16 = mybir.dt.bfloat16
AF = mybir.ActivationFunctionType
ALU = mybir.AluOpType

D = 2048
KD = 16
NT = 1024
NSEG = 4
SEG = 256
DEPTH = 4
D_FF = 5632
NJ = 44
N_IN = 11808
C_F, C_Q, C_K, C_V, C_GQ, C_GK, C_GV, C_GR, C_GAF, C_GAB, C_GT = (
    0, 1024, 2048, 2304, 2560, 3072, 3584, 4608, 5632, 5648, 5664)
EPS = 1e-6
WSLOT = 4096
NWS = 3
ND = 24

V_N1, V_N2, V_BADA, V_CW, V_CB, V_QN, V_KN, V_GN, V_FN = 0, 16, 32, 128, 392, 480, 481, 482, 484
NV = 500
B_ID, B_ONE, B_MF, B_MB, B_TF, B_TB, B_T2F, B_T2B, B_RM, B_CS, B_COS, B_SIN = (
    0, 128, 256, 768, 1280, 1408, 1536, 1664, 1792, 1920, 2176, 3200)
NB = 4224
S_CARRY, S_AB, S_C, S_ID = 0, 1, 41, 57
NS = 185


class T:
    __slots__ = ("name", "w", "r", "ps")

    def __init__(self, name, after=(), ps=False):
        self.name = name
        self.w = None
        self.r = []
        self.ps = ps
        for t in after:
            if t.w is not None:
                self.r.append(t.w)
            self.r.extend(t.r)


class Op:
    __slots__ = ("eng", "fn", "deps", "sig", "val", "dma", "dsem", "dval")


class K:
    def __init__(self, nc, depth):
        self.nc = nc
        self.L = depth
        self.ops = {e: [] for e in ("pe", "act", "dve", "pool", "sp")}
        self.dma_last = [None] * ND
        self.dma_cnt = [0] * ND
        self.dma_rr = {"sp": 0, "pool": ND // 2, "act": 0}
        self.ps_free = list(range(8))
        self.ws_rr = 0
        self.mod_state = {}

    def op(self, eng, fn, reads=(), writes=(), dma=False):
        o = Op()
        o.eng, o.fn, o.dma, o.sig, o.val = eng, fn, dma, False, 0
        deps = {}
        for t in reads:
            if t.w is not None:
                deps[id(t.w)] = t.w
            if t.ps:
                for r in t.r:
                    if r.eng != eng:
                        deps[id(r)] = r
        for t in writes:
            if t.w is not None:
                deps[id(t.w)] = t.w
            for r in t.r:
                deps[id(r)] = r
        if dma:
            h = ND // 2
            s = self.dma_rr[eng]
            lo = h if eng == "pool" else 0
            self.dma_rr[eng] = lo + (s - lo + 1) % h
            if self.dma_last[s] is not None:
                deps[id(self.dma_last[s])] = self.dma_last[s]
            self.dma_cnt[s] += 16
            o.dsem, o.dval = s, self.dma_cnt[s]
            self.dma_last[s] = o
        dl = []
        for d in deps.values():
            if (not d.dma) and d.eng == "pe" and eng == "pe" and not dma:
                continue
            dl.append(d)
            d.sig = True
        o.deps = dl
        for t in reads:
            if dma:
                t.r.append(o)
            else:
                t.r = [x for x in t.r if x.dma or x.eng != eng] + [o]
        for t in writes:
            t.w = o
            t.r = []
        self.ops[eng].append(o)
        return o

    def mod_tiles(self, l, n, part):
        if l >= self.L or l < 0:
            return
        key = (l, part)
        st = self.mod_state.get(key)
        if st is None:
            st = self.mod_state[key] = {"bank": self.psum(), "next": 0 if part == 0 else 16}
        if st["bank"] is None:
            return
        bm = st["bank"]
        end = 16 if part == 0 else 48
        w_ada = self.io["w_ada"][l]
        SCB = self.A["SCB"]
        for _ in range(n):
            j2 = st["next"]
            if j2 >= end:
                return
            st["next"] += 1
            wv, wt = self.wload(self.wsrc(w_ada, j2 * 256, 256), KD, 256)
            for mi in range(2):
                j = j2 * 2 + mi
                pairs = [(wv[:, k, mi * 128:(mi + 1) * 128], SCB[:, k:k + 1]) for k in range(KD)]
                self.mm(self.PS[:, bm, j:j + 1], pairs, [wt, self.tSCB], self.PT[bm])

    def mod_finish(self, l, part):
        self.mod_tiles(l, 48, part)
        st = self.mod_state[(l, part)]
        bm = st["bank"]
        MOD, VEC = self.A["MOD"], self.A["VEC"]
        tMOD, tVEC = self.tMODx, self.tVECx
        c0, c1 = (0, 32) if part == 0 else (32, 96)
        self.tt(MOD[:, c0:c1], self.PS[:, bm, c0:c1], VEC[:, V_BADA + c0:V_BADA + c1], ALU.add, [self.PT[bm], tVEC], [tMOD])
        self.psfree(bm)
        st["bank"] = None
        if part == 0:
            self.stt(MOD[:, 96:112], MOD[:, 16:32], 1.0, VEC[:, V_N1:V_N1 + 16], ALU.add, ALU.mult, [tMOD, tVEC], [tMOD])
        else:
            self.stt(MOD[:, 112:128], MOD[:, 64:80], 1.0, VEC[:, V_N2:V_N2 + 16], ALU.add, ALU.mult, [tMOD, tVEC], [tMOD])

    def psum(self, n=1):
        if n == 1:
            b = self.ps_free.pop(0)
            return b
        for i, b in enumerate(self.ps_free):
            if b % 2 == 0 and (b + 1) in self.ps_free:
                self.ps_free.remove(b)
                self.ps_free.remove(b + 1)
                return b
        raise RuntimeError("no psum pair")

    def psfree(self, b, n=1):
        for i in range(n):
            self.ps_free.append(b + i)

    def emit(self, block, sems, dsems):
        for e in self.ops:
            c = 0
            for o in self.ops[e]:
                if o.sig and not o.dma:
                    c += 1
                    o.val = c
        engs = {"pe": "tensor", "act": "scalar", "dve": "vector", "pool": "gpsimd", "sp": "sync"}
        for e, bn in engs.items():
            ops = self.ops[e]
            final = (e == "sp")

            def body(eng, ops=ops, e=e, final=final):
                waited = {}
                for o in ops:
                    for d in o.deps:
                        if d.dma:
                            key, sem, val = ("d", d.dsem), dsems[d.dsem], d.dval
                        else:
                            key, sem, val = ("e", d.eng), sems[d.eng], d.val
                        if waited.get(key, 0) < val:
                            eng.wait_ge(sem, val)
                            waited[key] = val
                    ins = o.fn(eng)
                    if o.dma:
                        ins.then_inc(dsems[o.dsem], 16)
                    elif o.sig:
                        ins.then_inc(sems[e], 1)
                if final:
                    for s in range(ND):
                        if self.dma_cnt[s] > 0:
                            eng.wait_ge(dsems[s], self.dma_cnt[s])
                    for en in ("pe", "act", "dve"):
                        n = sum(1 for o in self.ops[en] if o.sig)
                        if n:
                            eng.wait_ge(sems[en], n)

            getattr(block, bn)(body)

    def mm(self, out, pairs, reads, pst, start=True, stop=True):
        n = len(pairs)

        def fn(t):
            ins = None
            for i, (l, r) in enumerate(pairs):
                ins = t.matmul(out, lhsT=l, rhs=r, start=(start and i == 0), stop=(stop and i == n - 1))
            return ins
        return self.op("pe", fn, reads, [pst])

    def act(self, out, in_, func, reads, writes, bias=None, scale=None):
        kw = {}
        if bias is not None:
            kw["bias"] = bias
        if scale is not None:
            kw["scale"] = scale
        return self.op("act", lambda a: a.activation(out=out, in_=in_, func=func, **kw), reads, writes)

    def tt(self, out, a, b, op, reads, writes, eng="dve"):
        return self.op(eng, lambda v: v.tensor_tensor(out=out, in0=a, in1=b, op=op), reads, writes)

    def ts(self, out, a, s1, s2, op0, op1, reads, writes, eng="dve"):
        if op1 is None:
            return self.op(eng, lambda v: v.tensor_scalar(out=out, in0=a, scalar1=s1, scalar2=0.0, op0=op0, op1=ALU.add), reads, writes)
        return self.op(eng, lambda v: v.tensor_scalar(out=out, in0=a, scalar1=s1, scalar2=s2, op0=op0, op1=op1), reads, writes)

    def stt(self, out, a, s, b, op0, op1, reads, writes, eng="dve"):
        return self.op(eng, lambda v: v.scalar_tensor_tensor(out=out, in0=a, scalar=s, in1=b, op0=op0, op1=op1), reads, writes)

    def cp(self, out, in_, reads, writes, eng="dve"):
        if eng == "act":
            return self.op("act", lambda a: a.activation(out=out, in_=in_, func=AF.Copy), reads, writes)
        return self.op(eng, lambda v: v.tensor_copy(out=out, in_=in_), reads, writes)

    def dma(self, q, out, in_, reads, writes):
        return self.op(q, lambda e: e.dma_start(out=out, in_=in_), reads, writes, dma=True)

    def rstd(self, out, ps_ap, n, reads, writes):
        self.act(out, ps_ap, AF.Sqrt, reads, writes, bias=EPS, scale=1.0 / n)
        self.op("dve", lambda v: v.reciprocal(out=out, in_=out), writes, writes)

    def wload(self, src_ap, kc, cols):
        s = self.ws_rr
        self.ws_rr = (self.ws_rr + 1) % NWS
        view = self.WS[s][:, 0:kc * cols].rearrange("p (k c) -> p k c", k=kc)
        self.dma("pool", view, src_ap, [], [self.WT[s]])
        return view, self.WT[s]

    def wsrc(self, w_ap, c0, cols, k0=0, kc=None):
        v = w_ap.rearrange("(k p) c -> p k c", p=128)
        if kc is None:
            return v[:, :, c0:c0 + cols]
        return v[:, k0:k0 + kc, c0:c0 + cols]

    def proj_fm(self, w_ap, c0, nm, rhs, rhs_tiles, evac, kc=KD, mper=2, mrows=128):
        m = 0
        while m < nm:
            g = min(mper, nm - m)
            wv, wt = self.wload(self.wsrc(w_ap, c0 + m * mrows, g * mrows, 0, kc), kc, g * mrows)
            for mi in range(g):
                for half in range(2):
                    b = self.psum()
                    pap = self.PS[0:mrows, b, :]
                    pairs = [(wv[:, k, mi * mrows:(mi + 1) * mrows], rhs(k, half)) for k in range(kc)]
                    self.mm(pap, pairs, [wt] + rhs_tiles, self.PT[b])
                    evac(m + mi, half, pap, self.PT[b])
                    self.psfree(b)
            m += g

    def proj_tm(self, w_ap, c0, cols, lhs, lhs_tiles, evac, kc=KD):
        wv, wt = self.wload(self.wsrc(w_ap, c0, cols, 0, kc), kc, cols)
        for tc in range(8):
            b = self.psum()
            pap = self.PS[:, b, 0:cols]
            pairs = [(lhs(k, tc), wv[:, k, :]) for k in range(kc)]
            self.mm(pap, pairs, [wt] + lhs_tiles, self.PT[b])
            evac(tc, pap, self.PT[b])
            self.psfree(b)

    def build(self, io):
        nc = self.nc
        L = self.L
        self.io = io
        A = self.A
        X, H = A["X"], A["H"]
        self.WS = [A["W%d" % i] for i in range(NWS)]
        self.WT = [T("W%d" % i) for i in range(NWS)]
        self.PS = A["PS"]
        self.PT = [T("PS%d" % i, ps=True) for i in range(8)]
        XT = [T("X%d" % k) for k in range(KD)]
        HT = [T("H%d" % k) for k in range(KD)]
        CB, CS_, VEC, MOD = A["CB"], A["CST"], A["VEC"], A["MOD"]
        tCB, tCS, tVEC, tMOD = T("CB"), T("CST"), T("VEC"), T("MOD")
        self.tCBx, self.tCSx, self.tVECx = tCB, tCS, tVEC
        self.tMODx = tMOD
        self.tKST, self.tVST = [T("kst0"), T("kst1")], [T("vst0"), T("vst1")]
        AR = A["AR"]
        self.arena_tiles = []

        def bf(off, n):
            return AR[:, off:off + n]

        def f32v(off, n):
            return AR[:, off:off + n].bitcast(F32)

        ident = CB[:, B_ID:B_ID + 128]
        ones = CB[:, B_ONE:B_ONE + 128]
        identf = CS_[:, S_ID:S_ID + 128]
        carry = CS_[:, S_CARRY:S_CARRY + 1]

        self.dma("sp", X[:, :, :], io["x"][:, :, :], [], XT)
        self.dma("pool", CB[:, :], io["cb16"][:, :], [], [tCB])
        self.dma("sp", CS_[:, :], io["cst"][:, :], [], [tCS])
        SCB = A["SCB"]
        tSCB = T("SCB")
        self.tSCB = tSCB
        self.act(SCB[:, :], CS_[:, S_C:S_C + 16], AF.Silu, [tCS], [tSCB])

        def hr(k, half):
            return H[:, k, half * 512:(half + 1) * 512]

        def hl(k, tc):
            return H[:, k, tc * 128:(tc + 1) * 128]

        STOP = int(os.environ.get("KSTOP", "99"))
        self.tMG = T("mg_dummy")
        for l in range(L):
            w_ada, w_in = io["w_ada"][l], io["w_in"][l]
            if STOP < 1:
                break
            self.dma("sp", VEC[:, :], io["vecs"][l], [], [tVEC])
            self.mod_finish(l, 0)

            if STOP < 2:
                break
            self.norm_mod(X, XT, H, HT, MOD, tMOD, 96, 0, T("ar_n1", self.arena_tiles))
            self.arena_tiles = []
            if STOP < 3:
                break
            self.gla(l, hr, hl, HT)
            self.mod_finish(l, 1)
            if STOP < 4:
                break
            self.fnet(l, hr, hl, HT)
            if STOP < 5:
                break
            self.attn(l, hr, hl, HT)
            if STOP < 6:
                break
            MG = self.MG

            def ev3(m, half, pap, pt):
                xs = X[:, m, half * 512:(half + 1) * 512]
                self.stt(xs, pap, MOD[:, 32 + m:33 + m], xs, ALU.mult, ALU.add, [pt, tMOD, XT[m]], [XT[m]])
            self.proj_fm(io["w_out"][l], 0, 16, lambda k, half: MG[:, k, half * 512:(half + 1) * 512], [self.tMG], ev3)
            if STOP < 7:
                break
            self.norm_mod(X, XT, H, HT, MOD, tMOD, 112, 48, T("ar_n2", self.arena_tiles + [self.tMG]))
            self.arena_tiles = []
            if STOP < 8:
                break
            self.ffn(l, hr, HT, X, XT, MOD, tMOD)

        self.final(X, XT)

    def norm_mod(self, X, XT, H, HT, MOD, tMOD, gcol, shcol, tar):
        AR = self.A["AR"]
        SQ = [AR[:, i * 512:(i + 1) * 512] for i in range(4)]
        tSQ = [T("sq%d" % i, [tar]) for i in range(4)]
        RS = AR[:, 2048:4096].bitcast(F32)
        tRS = T("rs", [tar])
        T32 = [AR[:, 4096 + i * 2048:4096 + (i + 1) * 2048].bitcast(F32) for i in range(2)]
        tT32 = [T("t32%d" % i, [tar]) for i in range(2)]
        ones = self.A["CB"][:, B_ONE:B_ONE + 128]
        for half in range(2):
            b = self.psum()
            for k in range(KD):
                i = k % 4
                self.act(SQ[i], X[:, k, half * 512:(half + 1) * 512], AF.Square, [XT[k]], [tSQ[i]])
                self.mm(self.PS[:, b, :], [(ones, SQ[i])], [tSQ[i], self.tCBx], self.PT[b], start=(k == 0), stop=(k == KD - 1))
            self.rstd(RS[:, half * 512:(half + 1) * 512], self.PS[:, b, :], D, [self.PT[b]], [tRS])
            self.psfree(b)
        for k in range(KD):
            i = k % 2
            self.stt(T32[i], X[:, k, :], MOD[:, gcol + k:gcol + k + 1], RS, ALU.mult, ALU.mult, [XT[k], tMOD, tRS], [tT32[i]])
            self.act(H[:, k, :], T32[i], AF.Identity, [tT32[i], tMOD], [HT[k]], bias=MOD[:, shcol + k:shcol + k + 1])
        self.arena_tiles = tSQ + [tRS] + tT32

    def gla(self, l, hr, hl, HT):
        io, A = self.io, self.A
        AR, CB, CS_, VEC, MOD = A["AR"], A["CB"], A["CST"], A["VEC"], A["MOD"]
        tCB, tCS, tVEC = self.tCBx, self.tCSx, self.tVECx
        w_in = io["w_in"][l]
        prev = self.arena_tiles
        ones = CB[:, B_ONE:B_ONE + 128]
        carry = CS_[:, S_CARRY:S_CARRY + 1]
        O = AR[:, 16384:24576].rearrange("p (c t) -> p c t", c=8)
        tO = [T("O%d" % c, prev) for c in range(8)]
        base = 24576
        GAF = AR[0:17, 32768:34816].rearrange("p (a t) -> p a t", a=2)
        GW = AR[0:17, 34816:35840].rearrange("p (a t) -> p a t", a=2)
        tGAF = [T("gaf0", prev), T("gaf1", prev)]
        tGW = T("gw", prev)
        self.dma("pool", GW[:, :, :], io["gw"][l], [], [tGW])
        self.dma("pool", AR[16:17, 32768:34816].rearrange("p (a t) -> p a t", a=2), io["gones"][:, :, :], [], tGAF)
        for dr in range(2):
            def evg(m, half, pap, pt, dr=dr):
                self.cp(GAF[0:16, dr, half * 512:(half + 1) * 512], pap, [pt], [tGAF[dr]], eng="act")
            self.proj_fm(w_in, C_GAF + 16 * dr, 1, hr, HT, evg, mper=1, mrows=16)
        for hg in range(2):
            h0 = hg * 2
            pt0 = [T("gla_al%d" % hg, prev + self.arena_tiles)]
            GQ = AR[:, 0:2048].rearrange("p (h t) -> p h t", h=2)
            GK = AR[:, 2048:4096].rearrange("p (h t) -> p h t", h=2)
            GKT = AR[:, 4096:6144].rearrange("p (c d) -> p c d", c=8)
            GV = AR[:, 6144:10240].rearrange("p (c d) -> p c d", c=8)
            LSP = [AR[:, 10240 + d * 2048:10240 + (d + 1) * 2048].rearrange("p (c d) -> p c d", c=8) for d in range(2)]
            tGQ, tGK, tGKT, tGV = T("gq", pt0), T("gk", pt0), T("gkt", pt0), T("gv", pt0)
            tLSP = [T("lsp0", pt0), T("lsp1", pt0)]
            def evq(m, half, pap, pt):
                self.cp(GQ[:, m, half * 512:(half + 1) * 512], pap, [pt], [tGQ], eng="act")
            self.proj_fm(w_in, C_GQ + h0 * 128, 2, hr, HT, evq)

            def evk(m, half, pap, pt):
                self.cp(GK[:, m, half * 512:(half + 1) * 512], pap, [pt], [tGK], eng="dve")
            self.proj_fm(w_in, C_GK + h0 * 128, 2, hr, HT, evk)

            def evkt(tc, pap, pt):
                self.cp(GKT[:, tc, :], pap, [pt], [tGKT], eng="act")
            self.proj_tm(w_in, C_GK + h0 * 128, 256, hl, HT, evkt)
            for vv in range(2):
                def evv(tc, pap, pt, vv=vv):
                    self.cp(GV[:, tc, vv * 256:(vv + 1) * 256], pap, [pt], [tGV], eng="dve")
                self.proj_tm(w_in, C_GV + (h0 + vv) * 256, 256, hl, HT, evv)
            EX = f32_(AR, base, 512)
            tEX = T("ex", pt0)
            for dr in range(2):
                for tc in range(8):
                    b = self.psum()
                    pap = self.PS[:, b, 0:256]
                    self.mm(pap, [(GAF[0:17, dr, tc * 128:(tc + 1) * 128], GW[0:17, dr, h0 * 128:h0 * 128 + 256])],
                            [tGAF[dr], tGW], self.PT[b])
                    self.act(EX, pap, AF.Exp, [self.PT[b]], [tEX], scale=-1.0)
                    self.psfree(b)
                    self.act(LSP[dr][:, tc, :], EX, AF.Ln, [tEX], [tLSP[dr]], bias=1.0)
            S32 = f32_(AR, base + 512, 1024).rearrange("p (h d) -> p h d", h=2)
            SBF = AR[:, base + 1536:base + 2048].rearrange("p (h d) -> p h d", h=2)
            tS32, tSBF = T("s32", pt0), T("sbf", pt0)
            tEBL = T("ebl", pt0)
            tb = base + 2064
            ring = []
            for i in range(2):
                o = tb + i * 2560
                ring.append(dict(
                    QE=AR[:, o:o + 256].rearrange("p (h t) -> p h t", h=2), KE=AR[:, o + 256:o + 512].rearrange("p (h t) -> p h t", h=2),
                    KL=AR[:, o + 512:o + 768], AT=AR[:, o + 768:o + 1024].rearrange("p (h t) -> p h t", h=2),
                    EB=f32_(AR, o + 1024, 512), ENB=f32_(AR, o + 1536, 512), EKL=f32_(AR, o + 2048, 512),
                    EBL=f32_(AR, base + 2048 + 4 * i, 4), tEBL=T("ebl%d" % i, pt0),
                    t=[T("r%d_%d" % (i, q), pt0) for q in range(7)]))
            for dr in range(2):
                o_tf, o_t2, o_m = (B_TF, B_T2F, B_MF) if dr == 0 else (B_TB, B_T2B, B_MB)
                tri = CB[:, o_tf:o_tf + 128]
                tri2 = CB[:, o_t2:o_t2 + 128]
                mask = CB[:, o_m:o_m + 256].rearrange("p (h t) -> p h t", h=2)
                s0 = io["s0"][l, dr]
                self.dma("sp", S32[:, :, :], s0[:, h0:h0 + 2, :], [], [tS32])
                self.cp(SBF[:, :, :], S32[:, :, :], [tS32], [tSBF], eng="act")
                chunks = list(range(8)) if dr == 0 else list(range(7, -1, -1))
                lc = 127 if dr == 0 else 0

                def prep(ci, dr=dr, tri=tri, tri2=tri2, chunks=chunks, lc=lc):
                    c = chunks[ci]
                    R = ring[ci % 2]
                    tQE, tKE, tKL, tAT, tEB, tENB, tEKL = R["t"]
                    tsl = slice(c * 128, (c + 1) * 128)
                    b1 = self.psum()
                    for hh in range(2):
                        self.mm(self.PS[:, b1, hh * 128:(hh + 1) * 128], [(LSP[dr][:, c, hh * 128:(hh + 1) * 128], tri)],
                                [tLSP[dr], tCB], self.PT[b1])
                    self.act(R["EB"], self.PS[:, b1, 0:256], AF.Exp, [self.PT[b1]], [tEB])
                    self.act(R["ENB"], self.PS[:, b1, 0:256], AF.Exp, [self.PT[b1]], [tENB], scale=-1.0)
                    self.psfree(b1)
                    b2 = self.psum()
                    self.mm(self.PS[:, b2, 0:256], [(tri2, LSP[dr][:, c, :])], [tLSP[dr], tCB], self.PT[b2])
                    self.act(R["EKL"], self.PS[:, b2, 0:256], AF.Exp, [self.PT[b2]], [tEKL])
                    self.psfree(b2)
                    ebv = R["EB"].rearrange("p (h t) -> p h t", h=2)
                    enbv = R["ENB"].rearrange("p (h t) -> p h t", h=2)
                    self.stt(R["QE"], GQ[:, :, tsl], float(128 ** -0.5), ebv, ALU.mult, ALU.mult, [tGQ, tEB], [tQE])
                    self.tt(R["KE"], GK[:, :, tsl], enbv, ALU.mult, [tGK, tENB], [tKE])
                    self.tt(R["KL"], GKT[:, c, :], R["EKL"], ALU.mult, [tGKT, tEKL], [tKL])
                    self.cp(R["EBL"][:, 0:1], R["EB"][:, lc:lc + 1], [tEB], [R["tEBL"]], eng="dve")
                    self.cp(R["EBL"][:, 1:2], R["EB"][:, 128 + lc:128 + lc + 1], [tEB], [R["tEBL"]], eng="dve")

                def main(ci, dr=dr, mask=mask, chunks=chunks):
                    c = chunks[ci]
                    R = ring[ci % 2]
                    tQE, tKE, tKL, tAT, tEB, tENB, tEKL = R["t"]
                    EBL, tEBLr = R["EBL"], R["tEBL"]
                    tsl = slice(c * 128, (c + 1) * 128)
                    if ci > 0 and ci % 2 == 0:
                        self.ts(S32[:, :, :], S32[:, :, :], carry, None, ALU.mult, None, [tS32, tCS], [tS32])
                        self.cp(SBF[:, :, :], S32[:, :, :], [tS32], [tSBF], eng="act")
                    b3 = self.psum()
                    for hh in range(2):
                        self.mm(self.PS[:, b3, hh * 128:(hh + 1) * 128], [(R["KE"][:, hh, :], R["QE"][:, hh, :])],
                                [tKE, tQE], self.PT[b3])
                    self.tt(R["AT"], self.PS[:, b3, 0:256].rearrange("p (h t) -> p h t", h=2), mask, ALU.mult,
                            [self.PT[b3], tCB], [tAT])
                    self.psfree(b3)
                    b5 = self.psum()
                    for hh in range(2):
                        self.mm(self.PS[:, b5, hh * 256:(hh + 1) * 256], [(R["KL"][:, hh * 128:(hh + 1) * 128], GV[:, c, hh * 256:(hh + 1) * 256])],
                                [tKL, tGV], self.PT[b5])
                    b4 = self.psum()
                    for hh in range(2):
                        for e in range(2):
                            q = hh * 2 + e
                            self.mm(self.PS[:, b4, q * 128:(q + 1) * 128],
                                    [(SBF[:, hh, e * 128:(e + 1) * 128], R["QE"][:, hh, :]),
                                     (GV[:, c, hh * 256 + e * 128:hh * 256 + (e + 1) * 128], R["AT"][:, hh, :])],
                                    [tSBF, tQE, tGV, tAT], self.PT[b4])
                    for hh in range(2):
                        self.stt(S32[:, hh, :], S32[:, hh, :], EBL[:, hh:hh + 1], self.PS[:, b5, hh * 256:(hh + 1) * 256],
                                 ALU.mult, ALU.add, [tS32, tEBLr, self.PT[b5]], [tS32])
                    self.psfree(b5)
                    self.cp(SBF[:, :, :], S32[:, :, :], [tS32], [tSBF], eng="act")
                    ov = O[:, h0 * 2:h0 * 2 + 4, tsl]
                    pv = self.PS[:, b4, :].rearrange("p (q t) -> p q t", q=4)
                    tOs = tO[h0 * 2:h0 * 2 + 4]
                    if dr == 0:
                        self.cp(ov, pv, [self.PT[b4]], tOs, eng="act")
                    else:
                        self.tt(ov, pv, ov, ALU.add, [self.PT[b4]] + tOs, tOs)
                    self.psfree(b4)
                    if ci % 2 == 1:
                        seg = c // 2
                        dst = io["so"][dr][seg, l]
                        self.dma("sp", dst[h0:h0 + 2].rearrange("h p d -> p h d"), S32[:, :, :], [tS32], [])

                prep(0)
                for ci in range(8):
                    if ci + 1 < 8:
                        prep(ci + 1)
                    main(ci)
                    if l == 0 or ci % 2 == 0:
                        self.mod_tiles(l, 1, 1)
            self.arena_tiles = [tGQ, tGK, tGKT, tGV, tEX, tS32, tSBF, tEBL] + tLSP + [t for R in ring for t in R["t"]] + [R["tEBL"] for R in ring]
        pt1 = [T("gla_fin", self.arena_tiles)]
        OSQ = AR[:, 0:1024].rearrange("p (e t) -> p e t", e=2)
        tOSQ = T("osq", pt1)
        RSO = f32_(AR, 1024, 2048)
        tRSO = T("rso", pt1)
        SG = f32_(AR, 3072, 1024)
        tSG = T("sg", pt1)
        T3 = f32_(AR, 4096, 2048)
        tT3 = T("t3", pt1)
        for h in range(4):
            for half in range(2):
                hs = slice(half * 512, (half + 1) * 512)
                b = self.psum()
                for e in range(2):
                    self.act(OSQ[:, e, :], O[:, h * 2 + e, hs], AF.Square, [tO[h * 2 + e]], [tOSQ])
                self.mm(self.PS[:, b, :], [(ones, OSQ[:, 0, :]), (ones, OSQ[:, 1, :])], [tOSQ, tCB], self.PT[b])
                self.rstd(RSO[:, hs], self.PS[:, b, :], 256, [self.PT[b]], [tRSO])
                self.psfree(b)
            for e in range(2):
                c = h * 2 + e
                self.stt(T3, O[:, c, :], VEC[:, V_GN + e:V_GN + e + 1], RSO, ALU.mult, ALU.mult, [tO[c], tVEC, tRSO], [tT3])
                self.cp(O[:, c, :], T3, [tT3], [tO[c]], eng="act")
        def evr(m, half, pap, pt):
            hs = slice(half * 512, (half + 1) * 512)
            self.act(SG, pap, AF.Silu, [pt], [tSG])
            self.tt(O[:, m, hs], O[:, m, hs], SG, ALU.mult, [tO[m], tSG], [tO[m]])
        self.proj_fm(w_in, C_GR, 8, hr, HT, evr)
        self.arena_tiles = [tOSQ, tRSO, tSG, tT3]
        self.branch_out(l, io["w_gla"][l], lambda k, half: O[:, k, half * 512:(half + 1) * 512], tO, 2, hr, HT, first=True)
        self.arena_tiles = self.arena_tiles + tO + tGAF + [tGW]

    def branch_out(self, l, w_ap, rhs, rhs_tiles, gi, hr, HT, first=False):
        io, AR = self.io, self.A["AR"]
        w_in = io["w_in"][l]
        if first:
            self.tMGc = [T("mg%d" % m, self.arena_tiles) for m in range(KD)]
            self.tMG = T("mg_all")
        MG = AR[:, 0:16384].rearrange("p (k t) -> p k t", k=KD)
        self.MG = MG
        SGM = f32_(AR, 32768 - 4096, 2048).rearrange("p (a t) -> p a t", a=2)
        tSGM = [T("sgm0", self.arena_tiles), T("sgm1", self.arena_tiles)]
        for m2 in range(8):
            gv_, gt = self.wload(self.wsrc(w_in, C_GT + gi * D + m2 * 256, 256), KD, 256)
            bv_, bt = self.wload(self.wsrc(w_ap, m2 * 256, 256, 0, 8), 8, 256)
            for mi in range(2):
                m = m2 * 2 + mi
                for half in range(2):
                    hs = slice(half * 512, (half + 1) * 512)
                    b = self.psum()
                    self.mm(self.PS[:, b, :], [(gv_[:, k, mi * 128:(mi + 1) * 128], hr(k, half)) for k in range(KD)], [gt] + HT, self.PT[b])
                    self.act(SGM[:, half, :], self.PS[:, b, :], AF.Sigmoid, [self.PT[b]], [tSGM[half]])
                    self.psfree(b)
                    b = self.psum()
                    self.mm(self.PS[:, b, :], [(bv_[:, k, mi * 128:(mi + 1) * 128], rhs(k, half)) for k in range(8)], [bt] + rhs_tiles, self.PT[b])
                    if first:
                        self.tt(MG[:, m, hs], self.PS[:, b, :], SGM[:, half, :], ALU.mult, [self.PT[b], tSGM[half]], [self.tMGc[m], self.tMG])
                    else:
                        self.tt(SGM[:, half, :], self.PS[:, b, :], SGM[:, half, :], ALU.mult, [self.PT[b], tSGM[half]], [tSGM[half]])
                        self.tt(MG[:, m, hs], MG[:, m, hs], SGM[:, half, :], ALU.add, [self.tMGc[m], tSGM[half]], [self.tMGc[m], self.tMG])
                    self.psfree(b)
        self.arena_tiles = self.arena_tiles + tSGM

    def fnet(self, l, hr, hl, HT):
        io, A = self.io, self.A
        AR, CB = A["AR"], A["CB"]
        tCB = self.tCBx
        w_in = io["w_in"][l]
        prev = self.arena_tiles
        base = 16384
        FA = AR[:, base:base + 8192].rearrange("p (g t) -> p g t", g=8)
        tFA = [T("fa%d" % g, prev) for g in range(8)]
        FT = AR[:, base + 8192:base + 10240].rearrange("p (g t) -> p g t", g=2)
        tFT = [T("ft0", prev), T("ft1", prev)]
        XCS = AR[:, base + 10240:base + 14336].rearrange("p (c g d) -> p c g d", c=8, g=2)
        tXCS = T("xcs", prev)
        cs = CB[:, B_CS:B_CS + 256]
        for gp in range(4):
            def evf(m, half, pap, pt):
                self.cp(FT[:, m, half * 512:(half + 1) * 512], pap, [pt], [tFT[m]], eng="act")
            self.proj_fm(w_in, C_F + gp * 256, 2, hr, HT, evf)
            for g in range(2):
                for tc in range(8):
                    b = self.psum()
                    self.mm(self.PS[:, b, 0:256], [(FT[:, g, tc * 128:(tc + 1) * 128], cs)], [tFT[g], tCB], self.PT[b])
                    self.cp(XCS[:, tc, g, :], self.PS[:, b, 0:256], [self.PT[b]], [tXCS], eng="dve")
                    self.psfree(b)
            bs = [self.psum() for _ in range(4)]
            for tc in range(8):
                s = self.ws_rr
                self.ws_rr = (self.ws_rr + 1) % NWS
                cv = self.WS[s][:, 0:2048].rearrange("p (a t) -> p a t", a=2)
                self.dma("pool", cv, io["cts"][tc], [], [self.WT[s]])
                for g in range(2):
                    for half in range(2):
                        b = bs[g * 2 + half]
                        hs = slice(half * 512, (half + 1) * 512)
                        self.mm(self.PS[:, b, :], [(XCS[:, tc, g, 0:128], cv[:, 0, hs]), (XCS[:, tc, g, 128:256], cv[:, 1, hs])],
                                [tXCS, self.WT[s]], self.PT[b], start=(tc == 0), stop=(tc == 7))
            for g in range(2):
                for half in range(2):
                    b = bs[g * 2 + half]
                    self.cp(FA[:, gp * 2 + g, half * 512:(half + 1) * 512], self.PS[:, b, :], [self.PT[b]], [tFA[gp * 2 + g]],
                            eng=("act" if half == 0 else "dve"))
                    self.psfree(b)
        self.arena_tiles = tFT + [tXCS]
        self.branch_out(l, io["w_fourier"][l], lambda k, half: FA[:, k, half * 512:(half + 1) * 512], tFA, 0, hr, HT)
        self.arena_tiles = self.arena_tiles + tFA

    def attn(self, l, hr, hl, HT):
        io, A = self.io, self.A
        AR, CB, CS_, VEC = A["AR"], A["CB"], A["CST"], A["VEC"]
        tCB, tCS, tVEC = self.tCBx, self.tCSx, self.tVECx
        w_in = io["w_in"][l]
        prev = self.arena_tiles
        base = 16384
        ones = CB[:, B_ONE:B_ONE + 128]
        rm = CB[:, B_RM:B_RM + 128]
        QT = AR[:, base:base + 8192].rearrange("p (h t) -> p h t", h=8)
        tQT = [T("qt%d" % h, prev) for h in range(8)]
        KT = AR[:, base + 8192:base + 10752].rearrange("p (h s) -> p h s", h=2)
        tKT = [T("kt0", prev), T("kt1", prev)]
        VA = AR[:, base + 10752:base + 13312].rearrange("p (c d) -> p c d", c=10)
        tVA = T("va", prev)
        o = base + 13312
        identb = CB[:, B_ID:B_ID + 128]
        qr = []
        for r in range(2):
            ob = o + r * 2560
            qr.append(dict(ZSQ=AR[:, ob:ob + 512], RSQ=f32_(AR, ob + 512, 1024), QNB=AR[:, ob + 1536:ob + 2048],
                           TA=AR[:, ob + 2048:ob + 2560], LO=AR[:, ob + 2048:ob + 2560],
                           tZSQ=T("zsq%d" % r, prev), tRSQ=T("rsq%d" % r, prev), tQNB=T("qnb%d" % r, prev), tTA=T("ta%d" % r, prev)))
        tRSQ = qr[0]["tRSQ"]
        PTr = [AR[:, o + 5120 + i * 512:o + 5120 + (i + 1) * 512] for i in range(2)]
        tPTr = [T("ptr0", prev), T("ptr1", prev)]
        KST = A["KST"]
        VST = A["VST"]
        tKST, tVST = self.tKST, self.tVST
        self.dma("pool", KT[:, :, 0:256], io["ctxk"][l], [], tKT)
        self.dma("pool", VA[:, 0:2, :], io["ctxv"][l], [], [tVA])
        if int(os.environ.get("KSUB", "9")) < 0:
            return
        def evv(tc, pap, pt):
            self.cp(VA[:, 2 + tc, :], pap, [pt], [tVA], eng="act")
            self.cp(VST[:, tc % 2, :], pap, [pt], [tVST[tc % 2]], eng="dve")
            self.dma("sp", io["vout"][l][tc * 128:(tc + 1) * 128, :], VST[:, tc % 2, :], [tVST[tc % 2]], [])
        self.proj_tm(w_in, C_V, 256, hl, HT, evv)
        SUB = int(os.environ.get("KSUB", "9"))
        if SUB < 1:
            return
        wts = {}

        def front(u):
            hd, half = u // 2, u % 2
            isk = hd >= 8
            R = qr[u % 2]
            col = (C_K + (hd - 8) * 128) if isk else (C_Q + hd * 128)
            gcol = V_KN if isk else V_QN
            if half == 0:
                wts[hd] = self.wload(self.wsrc(w_in, col, 128), KD, 128)
            wv, wt = wts[hd]
            bp = self.psum()
            pap = self.PS[:, bp, :]
            self.mm(pap, [(wv[:, k, :], hr(k, half)) for k in range(KD)], [wt] + HT, self.PT[bp])
            self.act(R["ZSQ"], pap, AF.Square, [self.PT[bp]], [R["tZSQ"]])
            b = self.psum()
            self.mm(self.PS[:, b, :], [(ones, R["ZSQ"])], [R["tZSQ"], tCB], self.PT[b])
            self.rstd(R["RSQ"], self.PS[:, b, :], 128, [self.PT[b]], [R["tRSQ"]])
            self.psfree(b)
            self.stt(pap, pap, VEC[:, gcol:gcol + 1], R["RSQ"], ALU.mult, ALU.mult, [self.PT[bp], tVEC, R["tRSQ"]], [self.PT[bp]])
            self.cp(R["QNB"], pap, [self.PT[bp]], [R["tQNB"]], eng="act")
            if isk:
                kv = hd - 8
                ri = (kv * 2 + half) % 2
                self.tt(R["LO"], pap, R["QNB"], ALU.subtract, [self.PT[bp], R["tQNB"]], [R["tTA"]])
                for tq in range(4):
                    bb = self.psum()
                    self.mm(self.PS[:, bb, 0:128], [(R["QNB"][:, tq * 128:(tq + 1) * 128], identb), (R["LO"][:, tq * 128:(tq + 1) * 128], identb)],
                            [R["tQNB"], R["tTA"], tCB], self.PT[bb])
                    self.cp(KST[:, ri, tq, :], self.PS[:, bb, 0:128], [self.PT[bb]], [tKST[ri]], eng="dve")
                    self.psfree(bb)
                self.dma("sp", io["kout"][l][half * 512:(half + 1) * 512, kv * 128:(kv + 1) * 128].rearrange("(q p) d -> p q d", p=128),
                         KST[:, ri, :, :], [tKST[ri]], [])
            self.psfree(bp)

        def back(u):
            hd, half = u // 2, u % 2
            isk = hd >= 8
            R = qr[u % 2]
            hs = slice(half * 512, (half + 1) * 512)
            b = self.psum()
            pb = self.PS[:, b, :]
            self.mm(pb, [(rm, R["QNB"])], [R["tQNB"], tCB], self.PT[b])
            self.tt(R["TA"], R["QNB"], CB[:, B_COS + half * 512:B_COS + (half + 1) * 512], ALU.mult, [R["tQNB"], tCB], [R["tTA"]])
            self.tt(pb, pb, CB[:, B_SIN + half * 512:B_SIN + (half + 1) * 512], ALU.mult, [self.PT[b], tCB], [self.PT[b]])
            if isk:
                dst, dt_ = KT[:, hd - 8, 256 + half * 512:256 + (half + 1) * 512], tKT[hd - 8]
            else:
                dst, dt_ = QT[:, hd, hs], tQT[hd]
            self.tt(dst, pb, R["TA"], ALU.add, [self.PT[b], R["tTA"]], [dt_])
            self.psfree(b)

        front(0)
        for u in range(20):
            if u + 1 < 20:
                front(u + 1)
            back(u)
        if SUB < 2:
            return
        RD = f32_(AR, o + 512, 1024)
        scale = float(128 ** -0.5)
        for kv in range(2):
            for jp in range(2):
                hh = kv * 4 + jp * 2
                for g in range(NSEG):
                    gs = slice(g * SEG, (g + 1) * SEG)
                    qv = QT[:, hh:hh + 2, gs]
                    bo = self.psum()
                    bd = self.psum()
                    bsn = {}

                    def S(kc):
                        bsn[kc] = self.psum()
                        self.mm(self.PS[:, bsn[kc], :].rearrange("p (h t) -> p h t", h=2), [(KT[:, kv, kc * 128:(kc + 1) * 128], qv)],
                                [tKT[kv], tQT[hh], tQT[hh + 1]], self.PT[bsn[kc]])
                    S(0)
                    for kc in range(10):
                        if kc + 1 < 10:
                            S(kc + 1)
                        i = kc % 2
                        b_ = bsn.pop(kc)
                        self.act(PTr[i], self.PS[:, b_, :], AF.Exp, [self.PT[b_], tCS], [tPTr[i]], scale=scale,
                                 bias=CS_[:, S_AB + kc * 4 + g:S_AB + kc * 4 + g + 1])
                        self.psfree(b_)
                        self.mm(self.PS[:, bo, :], [(VA[:, kc, kv * 128:(kv + 1) * 128], PTr[i])], [tVA, tPTr[i]], self.PT[bo],
                                start=(kc == 0), stop=(kc == 9))
                        self.mm(self.PS[:, bd, :], [(ones, PTr[i])], [tCB, tPTr[i]], self.PT[bd], start=(kc == 0), stop=(kc == 9))
                    self.mod_tiles(l + 1, 1, 0)
                    self.op("dve", lambda v, bd=bd: v.reciprocal(out=RD, in_=self.PS[:, bd, :]), [self.PT[bd]], [tRSQ])
                    self.psfree(bd)
                    self.tt(qv, self.PS[:, bo, :].rearrange("p (h t) -> p h t", h=2), RD.rearrange("p (h t) -> p h t", h=2), ALU.mult,
                            [self.PT[bo], tRSQ], [tQT[hh], tQT[hh + 1]])
                    self.psfree(bo)
        self.arena_tiles = tKT + [tVA] + [R[k] for R in qr for k in ("tZSQ", "tRSQ", "tQNB", "tTA")] + tPTr
        if SUB < 3:
            return
        self.branch_out(l, io["w_attn"][l], lambda k, half: QT[:, k, half * 512:(half + 1) * 512], tQT, 1, hr, HT)
        self.arena_tiles = self.arena_tiles + tQT

    def ffn(self, l, hr, HT, X, XT, MOD, tMOD):
        io, A = self.io, self.A
        AR, CS_, VEC = A["AR"], A["CST"], A["VEC"]
        tCS, tVEC = self.tCSx, self.tVECx
        prev = self.arena_tiles
        carry = CS_[:, S_CARRY:S_CARRY + 1]
        ACT = AR[:, 0:22528].rearrange("p (j t) -> p j t", j=22)
        tACT = [T("act%d" % j, prev) for j in range(22)]
        o = 22528
        UP = [f32_(AR, o + s * 2080, 2064).rearrange("p (g t) -> p g t", g=4) for s in range(2)]
        tUP = [T("up0", prev), T("up1", prev)]
        AC = [f32_(AR, o + 4160 + s * 2048, 2048) for s in range(2)]
        tAC = [T("ac0", prev), T("ac1", prev)]
        SGT = f32_(AR, o + 8256, 2048)
        tSG = T("sgf", prev)
        for s in range(2):
            self.op("dve", lambda v, s=s: v.memset(UP[s][:, :, :], 0.0), [], [tUP[s]])
        w_up, w_dn = io["w_up"][l], io["w_down"][l]
        for sp in range(2):
            for jj in range(22):
                j = sp * 22 + jj
                wv, wt = self.wload(self.wsrc(w_up, j * 256, 256), KD, 256)
                for s in range(2):
                    for half in range(2):
                        b = self.psum()
                        self.mm(self.PS[:, b, :], [(wv[:, k, s * 128:(s + 1) * 128], hr(k, half)) for k in range(KD)], [wt] + HT, self.PT[b])
                        self.cp(UP[s][:, half * 2:half * 2 + 2, 1:257], self.PS[:, b, :].rearrange("p (g t) -> p g t", g=2),
                                [self.PT[b]], [tUP[s]], eng="act")
                        self.psfree(b)
                    self.ts(UP[s][:, 1:4, 0:1], UP[s][:, 0:3, 256:257], carry, None, ALU.mult, None, [tUP[s], tCS], [tUP[s]])
                    self.ts(UP[s][:, 0:3, 257:258], UP[s][:, 1:4, 1:2], carry, None, ALU.mult, None, [tUP[s], tCS], [tUP[s]])
                    cw = lambda t, s=s, j=j: VEC[:, V_CW + t * 88 + j * 2 + s:V_CW + t * 88 + j * 2 + s + 1]
                    acv = AC[s].rearrange("p (g t) -> p g t", g=4)
                    self.act(acv, UP[s][:, :, 1:257], AF.Identity, [tUP[s], tVEC], [tAC[s]], bias=VEC[:, V_CB + j * 2 + s:V_CB + j * 2 + s + 1],
                             scale=cw(1))
                    self.stt(acv, UP[s][:, :, 0:256], cw(0), acv, ALU.mult, ALU.add, [tUP[s], tVEC, tAC[s]], [tAC[s]])
                    self.stt(acv, UP[s][:, :, 2:258], cw(2), acv, ALU.mult, ALU.add, [tUP[s], tVEC, tAC[s]], [tAC[s]])
                self.act(SGT, AC[0], AF.Silu, [tAC[0]], [tSG])
                self.tt(ACT[:, jj, :], SGT, AC[1], ALU.mult, [tSG, tAC[1]], [tACT[jj]])
                if j % 3 == 0:
                    self.mod_tiles(l + 1, 1, 1)
            for mp in range(8):
                bs = [self.psum() for _ in range(4)]
                for jh in range(2):
                    wv, wt = self.wload(self.wsrc(w_dn, mp * 256, 256, sp * 22 + jh * 11, 11), 11, 256)
                    for mi in range(2):
                        for half in range(2):
                            b = bs[mi * 2 + half]
                            self.mm(self.PS[:, b, :], [(wv[:, k, mi * 128:(mi + 1) * 128], ACT[:, jh * 11 + k, half * 512:(half + 1) * 512]) for k in range(11)],
                                    [wt] + tACT[jh * 11:jh * 11 + 11], self.PT[b], start=(jh == 0), stop=(jh == 1))
                for mi in range(2):
                    m = mp * 2 + mi
                    for half in range(2):
                        b = bs[mi * 2 + half]
                        xs = X[:, m, half * 512:(half + 1) * 512]
                        self.stt(xs, self.PS[:, b, :], MOD[:, 80 + m:81 + m], xs, ALU.mult, ALU.add, [self.PT[b], tMOD, XT[m]], [XT[m]])
                        self.psfree(b)
        self.arena_tiles = tACT + tUP + tAC + [tSG]

    def final(self, X, XT):
        io, A = self.io, self.A
        AR, CB, VEC = A["AR"], A["CB"], A["VEC"]
        prev = self.arena_tiles
        tar = T("fin", prev)
        SQ = [AR[:, i * 512:(i + 1) * 512] for i in range(4)]
        tSQ = [T("fsq%d" % i, [tar]) for i in range(4)]
        RS = AR[:, 2048:4096].bitcast(F32)
        tRS = T("frs", [tar])
        ones = CB[:, B_ONE:B_ONE + 128]
        for half in range(2):
            b = self.psum()
            for k in range(KD):
                i = k % 4
                self.act(SQ[i], X[:, k, half * 512:(half + 1) * 512], AF.Square, [XT[k]], [tSQ[i]])
                self.mm(self.PS[:, b, :], [(ones, SQ[i])], [tSQ[i], self.tCBx], self.PT[b], start=(k == 0), stop=(k == KD - 1))
            self.rstd(RS[:, half * 512:(half + 1) * 512], self.PS[:, b, :], D, [self.PT[b]], [tRS])
            self.psfree(b)
        for k in range(KD):
            self.stt(X[:, k, :], X[:, k, :], VEC[:, V_FN + k:V_FN + k + 1], RS, ALU.mult, ALU.mult, [XT[k], self.tVECx, tRS], [XT[k]])
        self.dma("sp", io["y"][:, :, :], X[:, :, :], XT, [])


def f32_(AR, off, n):
    return AR[:, off:off + n].bitcast(F32)


NAR = 35840


def build_program(depth):
    nc = bass.Bass("TRN2", target_bir_lowering=False)
    L = depth
    io = {}

    def din(name, shape):
        io[name] = nc.dram_tensor(name, list(shape), F32, kind="ExternalInput").ap()

    def dout(name, shape):
        io[name] = nc.dram_tensor(name, list(shape), F32, kind="ExternalOutput").ap()

    din("x", [128, KD, NT])
    din("cb16", [128, NB])
    din("cst", [128, NS])
    din("vecs", [L, 128, NV])
    din("gw", [L, 17, 2, 512])
    din("gones", [1, 2, NT])
    din("cts", [8, 128, 2, NT])
    din("ctxk", [L, 128, 2, 256])
    din("ctxv", [L, 128, 2, 256])
    din("s0", [L, 2, 128, 4, 256])
    din("w_ada", [L, D, 6 * D])
    din("w_in", [L, D, N_IN])
    din("w_fourier", [L, 1024, D])
    din("w_attn", [L, 1024, D])
    din("w_gla", [L, 1024, D])
    din("w_out", [L, D, D])
    din("w_up", [L, D, 2 * D_FF])
    din("w_down", [L, D_FF, D])
    dout("y", [128, KD, NT])
    dout("kout", [L, NT, 256])
    dout("vout", [L, NT, 256])
    dout("sof", [NSEG, L, 4, 128, 256])
    dout("sob", [NSEG, L, 4, 128, 256])
    io["so"] = [io["sof"], io["sob"]]

    k = K(nc, L)
    names = [("X", [128, KD, NT], F32), ("H", [128, KD, NT], BF16)] + [("W%d" % i, [128, WSLOT], BF16) for i in range(NWS)] + [
        ("CB", [128, NB], BF16), ("CST", [128, NS], F32), ("VEC", [128, NV], F32), ("MOD", [128, 128], F32),
        ("SCB", [128, 16], BF16), ("KST", [128, 2, 4, 128], F32), ("VST", [128, 2, 256], F32), ("AR", [128, NAR], BF16)]
    with ExitStack() as es:
        A = {}
        for n, shp, dt_ in names:
            A[n] = es.enter_context(nc.sbuf_tensor(n, shp, dt_))
        A["PS"] = es.enter_context(nc.psum_tensor("PS", [128, 8, 512], F32))
        sems = {e: es.enter_context(nc.semaphore("s_" + e)) for e in ("pe", "act", "dve", "pool", "sp")}
        dsems = [es.enter_context(nc.semaphore("d%d" % i)) for i in range(ND)]
        k.A = A
        k.build(io)
        block = es.enter_context(nc.Block())
        k.emit(block, sems, dsems)
    return nc


def _consts_bf(kind):
    cb = np.zeros((128, NB), np.float32)
    cb[:, B_ID:B_ID + 128] = np.eye(128)
    cb[:, B_ONE:B_ONE + 128] = 1.0
    s = np.arange(128)[:, None]
    t = np.arange(128)[None, :]
    mf = (s <= t).astype(np.float32)
    mb = (s >= t).astype(np.float32)
    cb[:, B_MF:B_MF + 512] = np.tile(mf, (1, 4))
    cb[:, B_MB:B_MB + 512] = np.tile(mb, (1, 4))
    cb[:, B_TF:B_TF + 128] = -mf / 16.0
    cb[:, B_TB:B_TB + 128] = -mb / 16.0
    cb[:, B_T2F:B_T2F + 128] = -(s > t).astype(np.float32) / 16.0
    cb[:, B_T2B:B_T2B + 128] = -(s < t).astype(np.float32) / 16.0
    dd = np.arange(128)
    partner = np.where((dd % 64) < 32, dd + 32, dd - 32)
    rm = np.zeros((128, 128), np.float32)
    rm[partner, dd] = 1.0
    cb[:, B_RM:B_RM + 128] = rm
    c = np.arange(128)[:, None] * np.arange(128)[None, :]
    ang = 2 * np.pi * (c % 128) / 128.0
    cb[:, B_CS:B_CS + 128] = np.cos(ang)
    cb[:, B_CS + 128:B_CS + 256] = np.sin(ang)
    if kind == "sample":
        tt = np.arange(NT)
        row = (tt // 64).astype(np.float64)
        colp = (tt % 64).astype(np.float64)
        inv = 10000.0 ** (-np.arange(0, 64, 2, dtype=np.float64) / 64.0)
        cosT = np.zeros((128, NT))
        sinT = np.zeros((128, NT))
        for d in range(128):
            axis_pos = row if d < 64 else colp
            i = (d % 64) % 32
            a = axis_pos * inv[i]
            cosT[d] = np.cos(a)
            sinT[d] = np.sin(a) * (-1.0 if (d % 64) < 32 else 1.0)
        cb[:, B_COS:B_COS + NT] = cosT
        cb[:, B_SIN:B_SIN + NT] = sinT
    else:
        cb[:, B_COS:B_COS + NT] = 1.0
        cb[:, B_SIN:B_SIN + NT] = 0.0
    return cb


def _cts(kind):
    T_ = NT if kind == "sample" else SEG
    t = np.arange(T_)
    ang = 2 * np.pi * ((t[:, None] * t[None, :]) % T_) / T_
    sc = 1.0 / np.sqrt(T_ * 128.0)
    cm, sm = np.cos(ang) * sc, -np.sin(ang) * sc
    if kind != "sample":
        z = np.zeros((NT, NT))
        z2 = np.zeros((NT, NT))
        for g in range(NSEG):
            z[g * SEG:(g + 1) * SEG, g * SEG:(g + 1) * SEG] = cm
            z2[g * SEG:(g + 1) * SEG, g * SEG:(g + 1) * SEG] = sm
        cm, sm = z, z2
    out = np.stack([cm.reshape(8, 128, NT), sm.reshape(8, 128, NT)], axis=2)
    return np.ascontiguousarray(out.astype(np.float32))


def _fm(v, n):
    return np.ascontiguousarray(np.asarray(v, np.float32).reshape(n, 128).T)


_PROG = {}
_SIM_HOOK = None


def kernel(x_prompt, x_sample, cache_k, cache_v, state_gla_fwd, state_gla_bwd, c, c_ctx,
           w_ada, b_ada, norm1, w_in, q_norm, k_norm, w_fourier, w_attn, w_gate_f, b_gate_f,
           w_gate_b, b_gate_b, gla_norm, w_gla, w_out, norm2, w_up, conv_w, conv_b, w_down,
           final_norm):
    L = int(np.asarray(w_ada).shape[0])
    f = lambda a: np.asarray(a, np.float32)
    x_prompt, x_sample, cache_k, cache_v = f(x_prompt), f(x_sample), f(cache_k), f(cache_v)
    sgf, sgb, c, c_ctx = f(state_gla_fwd), f(state_gla_bwd), f(c), f(c_ctx)
    vecs = np.zeros((L, 128, NV), np.float32)
    gw = np.zeros((L, 17, 2, 512), np.float32)
    cw = f(conv_w).reshape(L, 3, 2, NJ, 128)
    cbv = f(conv_b).reshape(L, 2, NJ, 128)
    for l in range(L):
        vecs[l, :, V_N1:V_N1 + 16] = _fm(norm1[l], 16)
        vecs[l, :, V_N2:V_N2 + 16] = _fm(norm2[l], 16)
        vecs[l, :, V_BADA:V_BADA + 96] = _fm(b_ada[l], 96)
        for t in range(3):
            vecs[l, :, V_CW + t * 88:V_CW + (t + 1) * 88] = cw[l, t].transpose(2, 1, 0).reshape(128, 88)
        vecs[l, :, V_CB:V_CB + 88] = cbv[l].transpose(2, 1, 0).reshape(128, 88)
        vecs[l, :, V_QN] = f(q_norm)[l]
        vecs[l, :, V_KN] = f(k_norm)[l]
        vecs[l, :, V_GN:V_GN + 2] = _fm(gla_norm[l], 2)
        vecs[l, :, V_FN:V_FN + 16] = _fm(final_norm, 16)
        gw[l, 0:16, 0] = f(w_gate_f)[l]
        gw[l, 0:16, 1] = f(w_gate_b)[l]
        gw[l, 16, 0] = f(b_gate_f)[l]
        gw[l, 16, 1] = f(b_gate_b)[l]
    w_up_r = np.ascontiguousarray(f(w_up).reshape(L, D, 2, NJ, 128).transpose(0, 1, 3, 2, 4).reshape(L, D, 2 * D_FF))
    shared = dict(vecs=vecs, gw=gw, gones=np.ones((1, 2, NT), np.float32), w_ada=f(w_ada), w_in=f(w_in),
                  w_fourier=f(w_fourier), w_attn=f(w_attn), w_gla=f(w_gla), w_out=f(w_out), w_up=w_up_r, w_down=f(w_down))
    cbs = {kd: _consts_bf(kd) for kd in ("prompt", "sample")}
    ctss = {kd: _cts(kd) for kd in ("prompt", "sample")}
    in_maps = []
    for ci in range(8):
        kind = "prompt" if ci < 4 else "sample"
        m = dict(shared)
        if kind == "prompt":
            xt = x_prompt[4 * ci:4 * ci + 4].reshape(NT, D)
            cv = c_ctx
        else:
            b = ci - 4
            xt = x_sample[b]
            cv = c[b]
        m["x"] = np.ascontiguousarray(xt.reshape(NT, KD, 128).transpose(2, 1, 0))
        m["cb16"] = cbs[kind]
        m["cts"] = ctss[kind]
        cst = np.zeros((128, NS), np.float32)
        cst[:, S_CARRY] = 1.0 if kind == "sample" else 0.0
        ab = np.zeros((10, 4), np.float32)
        if kind == "prompt":
            ab[:] = -30000.0
            for kc in range(2, 10):
                ab[kc, (kc - 2) // 2] = 0.0
        cst[:, S_AB:S_AB + 40] = ab.reshape(1, 40)
        cst[:, S_C:S_C + 16] = _fm(cv, 16)
        cst[:, S_ID:S_ID + 128] = np.eye(128, dtype=np.float32)
        m["cst"] = cst
        if kind == "sample":
            b = ci - 4
            m["ctxk"] = np.ascontiguousarray(cache_k[b].transpose(0, 3, 2, 1))
            m["ctxv"] = np.ascontiguousarray(cache_v[b].reshape(L, 2, 128, 256).transpose(0, 2, 1, 3))
            m["s0"] = np.ascontiguousarray(np.stack([sgf[b], sgb[b]], axis=1).transpose(0, 1, 3, 2, 4))
        else:
            m["ctxk"] = np.zeros((L, 128, 2, 256), np.float32)
            m["ctxv"] = np.zeros((L, 128, 2, 256), np.float32)
            m["s0"] = np.zeros((L, 2, 128, 4, 256), np.float32)
        in_maps.append(m)
    if _SIM_HOOK is not None:
        return _SIM_HOOK(in_maps, L)
    if L not in _PROG:
        _PROG[L] = build_program(L)
    res = run_bass_kernel_spmd(_PROG[L], in_maps, core_ids=list(range(8)))
    R = res.results
    return _gather(R, L)


def _gather(R, L):
    untr = lambda y: np.asarray(y).transpose(2, 1, 0).reshape(NT, D)
    y_prompt = np.stack([untr(R[ci]["y"]) for ci in range(4)]).reshape(16, SEG, D)
    y_sample = np.stack([untr(R[ci]["y"]) for ci in range(4, 8)])
    nk = np.concatenate([np.asarray(R[ci]["kout"]).reshape(L, 4, SEG, 2, 128).transpose(1, 0, 2, 3, 4) for ci in range(4)], axis=0)
    nv = np.concatenate([np.asarray(R[ci]["vout"]).reshape(L, 4, SEG, 2, 128).transpose(1, 0, 2, 3, 4) for ci in range(4)], axis=0)
    nsf = np.concatenate([np.asarray(R[ci]["sof"]) for ci in range(4)], axis=0)
    nsb = np.concatenate([np.asarray(R[ci]["sob"]) for ci in range(4)], axis=0)
    return (y_prompt.astype(np.float32), y_sample.astype(np.float32), np.ascontiguousarray(nk, np.float32),
            np.ascontiguousarray(nv, np.float32), np.ascontiguousarray(nsf, np.float32), np.ascontiguousarray(nsb, np.float32))
```
